# Optimizing a Trainium2 kernel written in Bass

```python
import jax, jax.numpy as jnp
from jax import lax
import numpy as np

D_MODEL = 1024
BATCH = 16
SEQ = 4096
DEPTH = 4

GRID_W = 64
CTX_LEN = 256
N_EVEN = (DEPTH + 1) // 2
N_ODD = DEPTH // 2
N_SUB = 3
D_FF = 2816
HEAD_DIM = 64
NA_HEADS = 12
NA_WIN_ROWS = 8
NA_WIN_COLS = 16
NA_QCOL_BLOCK = 16
NA_KCOL_BLOCK = 32
NA_W = NA_HEADS * HEAD_DIM
FNET_GROUPS = 4
FNET_GROUP_DIM = 64
FNET_W = FNET_GROUPS * FNET_GROUP_DIM
IN_EVEN = 3 * NA_W + FNET_W
MIX_EVEN = NA_W + FNET_W
GQA_Q_HEADS = 8
GQA_KV_HEADS = 2
GQA_Q_W = GQA_Q_HEADS * HEAD_DIM
GQA_KV_W = GQA_KV_HEADS * HEAD_DIM
SGU_GROUPS = 4
SGU_CHUNK = 128
SGU_GROUP_DIM = 128
SGU_W = SGU_GROUPS * SGU_GROUP_DIM
IN_ODD = GQA_Q_W + 2 * GQA_KV_W + 2 * SGU_W
MIX_ODD = GQA_Q_W + SGU_W
Q_BLOCK = 128
ROPE_THETA = 10000.0
EPS = 1e-6
NEG_INF = -1e30

kernel_name = "hybrid_natten_fnet_gqa_sgu_macaron_dit"


def rms_norm(x, g):
    x32 = x.astype(jnp.float32)
    y = x32 * lax.rsqrt(jnp.mean(x32 * x32, axis=-1, keepdims=True) + EPS)
    return (y * g.astype(jnp.float32)).astype(x.dtype)


def modulate(h, shift, scale):
    return h * (1 + scale) + shift


def swiglu(h, w_gu, w_down):
    gate, up = jnp.split(h @ w_gu, 2, axis=-1)
    return (jax.nn.silu(gate) * up) @ w_down


def macaron_ffn(x, g, shift, scale, gate, w_gu, w_down):
    h = modulate(rms_norm(x, g), shift, scale)
    return x + 0.5 * gate * swiglu(h, w_gu, w_down)


def heads(z, n):
    return z.reshape(z.shape[:-1] + (n, HEAD_DIM))


def axial_rope_tables(n_tokens):
    t = jnp.arange(n_tokens, dtype=jnp.int32)
    n_freq = HEAD_DIM // 4
    inv_freq = ROPE_THETA ** (-jnp.arange(n_freq, dtype=jnp.float32) / n_freq)
    row = (t // GRID_W).astype(jnp.float32)[:, None] * inv_freq
    col = (t % GRID_W).astype(jnp.float32)[:, None] * inv_freq
    ang = jnp.concatenate([row, col], axis=-1)
    return jnp.cos(ang), jnp.sin(ang)


def apply_rope(x, cos, sin):
    xf = x.astype(jnp.float32).reshape(x.shape[:-1] + (HEAD_DIM // 2, 2))
    x0, x1 = xf[..., 0], xf[..., 1]
    c = cos[None, :, None, :]
    s = sin[None, :, None, :]
    out = jnp.stack([x0 * c - x1 * s, x0 * s + x1 * c], axis=-1)
    return out.reshape(x.shape).astype(x.dtype)


def gqa_attend(q, k, v):
    b, tq, hq, dh = q.shape
    hkv = k.shape[2]
    qg = q.reshape(b, tq, hkv, hq // hkv, dh)
    s = jnp.einsum('bqkgd,btkd->bkgqt', qg, k).astype(jnp.float32) * (dh ** -0.5)
    p = jax.nn.softmax(s, axis=-1).astype(v.dtype)
    o = jnp.einsum('bkgqt,btkd->bqkgd', p, v)
    return o.reshape(b, tq, hq * dh)


def blocked_gqa_attend(q, k, v):
    b, s, hq, dh = q.shape
    nb = s // Q_BLOCK
    qb = q.reshape(b, nb, Q_BLOCK, hq, dh).transpose(1, 0, 2, 3, 4)
    out = lax.map(lambda qi: gqa_attend(qi, k, v), qb)
    return out.transpose(1, 0, 2, 3).reshape(b, s, hq * dh)


def neighbourhood_attend(q, k, v, k_ctx, v_ctx, rpb):
    b, s, h, dh = q.shape
    rows = s // GRID_W
    kr = min(NA_WIN_ROWS, rows)
    n_cb = GRID_W // NA_QCOL_BLOCK
    qcol = np.arange(GRID_W).reshape(n_cb, NA_QCOL_BLOCK)
    col_start = np.clip(qcol - NA_WIN_COLS // 2, 0, GRID_W - NA_WIN_COLS)
    kcol0 = np.clip(np.arange(n_cb) * NA_QCOL_BLOCK - NA_WIN_COLS // 2, 0, GRID_W - NA_KCOL_BLOCK)
    kcol = kcol0[:, None] + np.arange(NA_KCOL_BLOCK)
    col_ok = ((kcol[:, None, :] >= col_start[:, :, None]) &
              (kcol[:, None, :] < col_start[:, :, None] + NA_WIN_COLS))
    dcol_idx = np.clip(kcol[:, None, :] - qcol[:, :, None] + NA_WIN_COLS - 1, 0, 2 * NA_WIN_COLS - 2)
    row_start = np.clip(np.arange(rows) - kr // 2, 0, rows - kr).astype(np.int32)
    mask = jnp.asarray(col_ok[:, :, None, :])
    bias_col = rpb[:, :, dcol_idx]
    kg = k.reshape(b, rows, GRID_W, h, dh)
    vg = v.reshape(b, rows, GRID_W, h, dh)
    qg = q.reshape(b, rows, n_cb, NA_QCOL_BLOCK, h, dh).transpose(1, 0, 2, 3, 4, 5)
    scale = dh ** -0.5
    n_loc = kr * NA_KCOL_BLOCK

    def one_row(args):
        q_row, r, r0 = args
        k_nb = lax.dynamic_slice_in_dim(kg, r0, kr, axis=1)[:, :, kcol]
        v_nb = lax.dynamic_slice_in_dim(vg, r0, kr, axis=1)[:, :, kcol]
        s_loc = jnp.einsum('bjqhd,brjkhd->bhjqrk', q_row, k_nb).astype(jnp.float32) * scale
        drow_idx = r0 + jnp.arange(kr, dtype=jnp.int32) - r + NA_WIN_ROWS - 1
        bias = bias_col[:, drow_idx].transpose(0, 2, 3, 1, 4)
        s_loc = jnp.where(mask, s_loc + bias[None].astype(jnp.float32), NEG_INF)
        s_ctx = jnp.einsum('bjqhd,bthd->bhjqt', q_row, k_ctx).astype(jnp.float32) * scale
        sc = jnp.concatenate([s_loc.reshape(b, h, n_cb, NA_QCOL_BLOCK, n_loc), s_ctx], axis=-1)
        p = jax.nn.softmax(sc, axis=-1).astype(v.dtype)
        p_loc = p[..., :n_loc].reshape(b, h, n_cb, NA_QCOL_BLOCK, kr, NA_KCOL_BLOCK)
        o = (jnp.einsum('bhjqrk,brjkhd->bjqhd', p_loc, v_nb)
             + jnp.einsum('bhjqt,bthd->bjqhd', p[..., n_loc:], v_ctx))
        return o.reshape(b, GRID_W, h * dh)

    out = lax.map(one_row, (qg, jnp.arange(rows, dtype=jnp.int32), jnp.asarray(row_start)))
    return out.transpose(1, 0, 2, 3).reshape(b, s, h * dh)


def fourier_mix(f):
    b, t, _ = f.shape
    z = f.astype(jnp.float32).reshape(b, t, FNET_GROUPS, FNET_GROUP_DIM)
    y = jnp.fft.fft2(z, axes=(1, 3), norm="ortho").real
    return y.reshape(b, t, FNET_W).astype(f.dtype)


def spatial_gating(u, v, v_g, w_s, b_s):
    b, t, _ = u.shape
    u = jax.nn.gelu(u)
    v = rms_norm(jax.nn.gelu(v).reshape(b, t, SGU_GROUPS, SGU_GROUP_DIM),
                 v_g.reshape(SGU_GROUPS, SGU_GROUP_DIM))
    vc = v.reshape(b, t // SGU_CHUNK, SGU_CHUNK, SGU_GROUPS, SGU_GROUP_DIM)
    mixed = jnp.einsum('gij,bnjgc->bnigc', w_s, vc) + b_s.T[None, None, :, :, None]
    return u * mixed.reshape(b, t, SGU_W)


def mixer_ab(hl, hc, w_in, w_out, qk_g, rpb, with_ctx_out):
    shp_l = hl.shape[:2] + (NA_HEADS, HEAD_DIM)
    ql, kl, vl, fl = jnp.split(hl @ w_in, [NA_W, 2 * NA_W, 3 * NA_W], axis=-1)
    ql = rms_norm(ql.reshape(shp_l), qk_g[0])
    kl = rms_norm(kl.reshape(shp_l), qk_g[1])
    vl = vl.reshape(shp_l)
    if with_ctx_out:
        qc, kc, vc, fc = jnp.split(hc @ w_in, [NA_W, 2 * NA_W, 3 * NA_W], axis=-1)
    else:
        kc, vc = jnp.split(hc @ w_in[:, NA_W:3 * NA_W], 2, axis=-1)
    kc = rms_norm(heads(kc, NA_HEADS), qk_g[1])
    vc = heads(vc, NA_HEADS)
    att_l = neighbourhood_attend(ql, kl, vl, kc, vc, rpb)
    out_l = jnp.concatenate([att_l, fourier_mix(fl)], axis=-1) @ w_out
    if not with_ctx_out:
        return out_l, None
    qc = rms_norm(heads(qc, NA_HEADS), qk_g[0])
    att_c = gqa_attend(qc, kc, vc)
    out_c = jnp.concatenate([att_c, fourier_mix(fc)], axis=-1) @ w_out
    return out_l, out_c


def mixer_cd(hl, hc, w_in, w_out, qk_g, v_g, w_s, b_s, with_ctx_out):
    o_k = GQA_Q_W
    o_v = o_k + GQA_KV_W
    o_u = o_v + GQA_KV_W
    o_s = o_u + SGU_W
    ql, kl, val_l, ul, sl = jnp.split(hl @ w_in, [o_k, o_v, o_u, o_s], axis=-1)
    cos, sin = axial_rope_tables(hl.shape[1])
    ql = apply_rope(rms_norm(heads(ql, GQA_Q_HEADS), qk_g[0]), cos, sin)
    kl = apply_rope(rms_norm(heads(kl, GQA_KV_HEADS), qk_g[1]), cos, sin)
    val_l = heads(val_l, GQA_KV_HEADS)
    if with_ctx_out:
        qc, kc, val_c, uc, sc = jnp.split(hc @ w_in, [o_k, o_v, o_u, o_s], axis=-1)
    else:
        kc, val_c = jnp.split(hc @ w_in[:, o_k:o_u], 2, axis=-1)
    kc = rms_norm(heads(kc, GQA_KV_HEADS), qk_g[1])
    val_c = heads(val_c, GQA_KV_HEADS)
    att_l = blocked_gqa_attend(ql, jnp.concatenate([kl, kc], axis=1),
                               jnp.concatenate([val_l, val_c], axis=1))
    out_l = jnp.concatenate([att_l, spatial_gating(ul, sl, v_g, w_s, b_s)], axis=-1) @ w_out
    if not with_ctx_out:
        return out_l, None
    qc = rms_norm(heads(qc, GQA_Q_HEADS), qk_g[0])
    att_c = gqa_attend(qc, kc, val_c)
    out_c = jnp.concatenate([att_c, spatial_gating(uc, sc, v_g, w_s, b_s)], axis=-1) @ w_out
    return out_l, out_c


def setup_inputs(seed: int = 0) -> dict:
    key = jax.random.key(seed)
    ks = jax.random.split(key, 20)
    D = D_MODEL

    def nrm(k, shape, scale):
        return jax.random.normal(k, shape, jnp.float32) * scale

    return {
        "x": nrm(ks[0], (BATCH, SEQ, D), 1.0),
        "c": nrm(ks[1], (BATCH, D), 1.0),
        "ctx": nrm(ks[2], (BATCH, CTX_LEN, D), 1.0),
        "c_ctx": nrm(ks[3], (D,), 1.0),
        "norm_g": 1.0 + nrm(ks[4], (DEPTH, N_SUB, D), 0.02),
        "w_mod": nrm(ks[5], (DEPTH, D, N_SUB * 3 * D), 0.5 * D ** -0.5),
        "b_mod": nrm(ks[6], (DEPTH, N_SUB * 3 * D), 0.02),
        "ffn_w_gu": nrm(ks[7], (DEPTH, 2, D, 2 * D_FF), D ** -0.5),
        "ffn_w_down": nrm(ks[8], (DEPTH, 2, D_FF, D), D_FF ** -0.5),
        "w_in_ab": nrm(ks[9], (N_EVEN, D, IN_EVEN), D ** -0.5),
        "w_out_ab": nrm(ks[10], (N_EVEN, MIX_EVEN, D), MIX_EVEN ** -0.5),
        "qk_g_a": 1.0 + nrm(ks[11], (N_EVEN, 2, HEAD_DIM), 0.02),
        "rpb_a": nrm(ks[12], (N_EVEN, NA_HEADS, 2 * NA_WIN_ROWS - 1, 2 * NA_WIN_COLS - 1), 0.05),
        "w_in_cd": nrm(ks[13], (N_ODD, D, IN_ODD), D ** -0.5),
        "w_out_cd": nrm(ks[14], (N_ODD, MIX_ODD, D), MIX_ODD ** -0.5),
        "qk_g_d": 1.0 + nrm(ks[15], (N_ODD, 2, HEAD_DIM), 0.02),
        "v_g_c": 1.0 + nrm(ks[16], (N_ODD, SGU_W), 0.02),
        "w_s_c": nrm(ks[17], (N_ODD, SGU_GROUPS, SGU_CHUNK, SGU_CHUNK), SGU_CHUNK ** -0.5),
        "b_s_c": 1.0 + nrm(ks[18], (N_ODD, SGU_GROUPS, SGU_CHUNK), 0.02),
    }


def reference(x, c, ctx, c_ctx, norm_g, w_mod, b_mod, ffn_w_gu, ffn_w_down,
              w_in_ab, w_out_ab, qk_g_a, rpb_a, w_in_cd, w_out_cd, qk_g_d, v_g_c, w_s_c, b_s_c):
    b = x.shape[0]
    xl, xc = x, ctx
    silu_c = jax.nn.silu(c)
    silu_cc = jax.nn.silu(c_ctx)
    for layer in range(DEPTH):
        mod_l = (silu_c @ w_mod[layer] + b_mod[layer]).reshape(b, N_SUB, 3, 1, D_MODEL)
        mod_c = (silu_cc @ w_mod[layer] + b_mod[layer]).reshape(N_SUB, 3, D_MODEL)
        with_ctx_out = layer < DEPTH - 1
        xl = macaron_ffn(xl, norm_g[layer, 0], mod_l[:, 0, 0], mod_l[:, 0, 1], mod_l[:, 0, 2],
                         ffn_w_gu[layer, 0], ffn_w_down[layer, 0])
        xc = macaron_ffn(xc, norm_g[layer, 0], mod_c[0, 0], mod_c[0, 1], mod_c[0, 2],
                         ffn_w_gu[layer, 0], ffn_w_down[layer, 0])
        hl = modulate(rms_norm(xl, norm_g[layer, 1]), mod_l[:, 1, 0], mod_l[:, 1, 1])
        hc = modulate(rms_norm(xc, norm_g[layer, 1]), mod_c[1, 0], mod_c[1, 1])
        if layer % 2 == 0:
            i = layer // 2
            out_l, out_c = mixer_ab(hl, hc, w_in_ab[i], w_out_ab[i], qk_g_a[i], rpb_a[i], with_ctx_out)
        else:
            i = layer // 2
            out_l, out_c = mixer_cd(hl, hc, w_in_cd[i], w_out_cd[i], qk_g_d[i], v_g_c[i],
                                    w_s_c[i], b_s_c[i], with_ctx_out)
        xl = xl + mod_l[:, 1, 2] * out_l
        xl = macaron_ffn(xl, norm_g[layer, 2], mod_l[:, 2, 0], mod_l[:, 2, 1], mod_l[:, 2, 2],
                         ffn_w_gu[layer, 1], ffn_w_down[layer, 1])
        if with_ctx_out:
            xc = xc + mod_c[1, 2] * out_c
            xc = macaron_ffn(xc, norm_g[layer, 2], mod_c[2, 0], mod_c[2, 1], mod_c[2, 2],
                             ffn_w_gu[layer, 1], ffn_w_down[layer, 1])
    return xl
```

```python
import contextlib
import numpy as np
import ml_dtypes
import concourse.bass as bass
import concourse.mybir as mybir
from concourse.bass_utils import run_bass_kernel_spmd

F32 = mybir.dt.float32
BF16 = mybir.dt.bfloat16
AF = mybir.ActivationFunctionType
ALU = mybir.AluOpType
NPBF = ml_dtypes.bfloat16

P = 128
D = 1024
KC = 8
NB = 2
L = 4096
C = 256
TN = 512
NLAT = NB * L
NTOK = NLAT + NB * C
NT = NTOK // TN
NTL = NLAT // TN
DFF = 2816
FC = DFF // P
DEPTH = 4
EPS = 1e-6
HD = 64
GRID = 64
NA_HEADS = 12
NA_W = 768
IN_EVEN = 2560
IN_ODD = 1792

SEM_EPOCH = 24000
DMA_POOL = 8


class Prog:
    ENGS = ("pe", "act", "dve", "pool", "sp")

    def __init__(self, nc):
        self.nc = nc
        self.ops = []
        self.state = {}
        self.last_op = {}
        self.dma_since_barrier = []
        self._uid = 0

    def uid(self, s):
        self._uid += 1
        return "%s_%d" % (s, self._uid)

    def add(self, eng, fn, reads=(), writes=(), dma=False):
        idx = len(self.ops)
        deps = set()
        for k in reads:
            st = self.state.get(k)
            if st is not None and st[0] is not None:
                deps.add(st[0])
        for k in writes:
            st = self.state.get(k)
            if st is not None:
                if st[0] is not None:
                    deps.add(st[0])
                deps.update(st[1].values())
                deps.update(st[2])
        for k in reads:
            st = self.state.get(k)
            if st is None:
                st = [None, {}, []]
                self.state[k] = st
            if dma:
                st[2].append(idx)
            else:
                st[1][eng] = idx
        for k in writes:
            self.state[k] = [idx, {}, []]
        deps.discard(idx)
        self.ops.append(dict(eng=eng, fn=fn, deps=deps, dma=dma))
        if dma:
            self.dma_since_barrier.append(idx)
        else:
            self.last_op[eng] = idx
        return idx

    def barrier(self):
        b = set(self.last_op.values()) | set(self.dma_since_barrier)
        self.dma_since_barrier = []
        for e in self.ENGS:
            idx = len(self.ops)
            self.ops.append(dict(eng=e, fn=None, deps=set(b), dma=False))
        self.last_op = {}

    def emit(self, stack):
        nc = self.nc
        ops = self.ops
        needed = set()
        for op in ops:
            needed |= op["deps"]
        eng_sems = {e: [] for e in self.ENGS}
        eng_cnt = {e: 0 for e in self.ENGS}
        dma_pool = {e: [] for e in self.ENGS}
        dma_rr = {e: 0 for e in self.ENGS}

        def new_sem(tag):
            return stack.enter_context(nc.semaphore(self.uid(tag)))

        for idx, op in enumerate(ops):
            e = op["eng"]
            if op["dma"]:
                pool = dma_pool[e]
                if len(pool) < DMA_POOL:
                    pool.append([new_sem("dq_" + e), 0])
                slot = dma_rr[e] % DMA_POOL
                dma_rr[e] += 1
                if pool[slot][1] * 16 + 16 > SEM_EPOCH:
                    pool[slot] = [new_sem("dq_" + e), 0]
                ent = pool[slot]
                op["prev"] = (ent[0], ent[1] * 16) if ent[1] > 0 else None
                ent[1] += 1
                op["sig"] = (ent[0], ent[1] * 16, None)
            elif idx in needed and op["fn"] is not None:
                if not eng_sems[e] or eng_cnt[e] >= SEM_EPOCH:
                    eng_sems[e].append(new_sem("es_" + e))
                    eng_cnt[e] = 0
                eng_cnt[e] += 1
                op["sig"] = (eng_sems[e][-1], eng_cnt[e], (e, len(eng_sems[e]) - 1))
            else:
                op["sig"] = None

        engobj = dict(pe=nc.tensor, act=nc.scalar, dve=nc.vector, pool=nc.gpsimd, sp=nc.sync)
        block = stack.enter_context(nc.Block())

        def run_engine(e, eobj):
            known = {}
            knowne = {}

            def wait(sig):
                sem, val, ek = sig
                if ek is not None:
                    pe_, ep = ek
                    cur = knowne.get(pe_, (-1, 0))
                    if (ep, val) <= cur:
                        return
                    knowne[pe_] = (ep, val)
                else:
                    if known.get(id(sem), 0) >= val:
                        return
                    known[id(sem)] = val
                eobj.wait_ge(sem, val)

            for idx, op in enumerate(ops):
                if op["eng"] != e:
                    continue
                for d in sorted(op["deps"]):
                    dop = ops[d]
                    if dop["sig"] is None:
                        continue
                    if (not dop["dma"]) and dop["eng"] == "pe" and e == "pe" and not op["dma"]:
                        continue
                    wait(dop["sig"])
                if op["dma"] and op["prev"] is not None:
                    sem, val = op["prev"]
                    if known.get(id(sem), 0) < val:
                        known[id(sem)] = val
                        eobj.wait_ge(sem, val)
                if op["fn"] is None:
                    continue
                ins = op["fn"](eobj)
                if op["sig"] is not None:
                    sem, val, ek = op["sig"]
                    ins.then_inc(sem, 16 if op["dma"] else 1)

        block.tensor(lambda t: run_engine("pe", t))
        block.scalar(lambda t: run_engine("act", t))
        block.vector(lambda t: run_engine("dve", t))
        block.gpsimd(lambda t: run_engine("pool", t))
        block.sync(lambda t: run_engine("sp", t))


class Builder:
    def __init__(self, debug=False, stages=None, nlayers=DEPTH):
        self.debug = debug
        self.stages = stages
        self.nlayers = nlayers
        self.nc = bass.Bass("TRN2", target_bir_lowering=False)
        self.pg = Prog(self.nc)
        self.inputs = {}
        self.scratch = {}

    def dram_in(self, name, shape, dt):
        t = self.nc.dram_tensor(name, list(shape), dt, kind="ExternalInput")
        self.inputs[name] = t
        return t.ap()

    def dram_scratch(self, name, shape, dt):
        if self.debug:
            t = self.nc.dram_tensor(name, list(shape), dt, kind="ExternalOutput")
        else:
            t = self.nc.dram_tensor(name, list(shape), dt)
        self.scratch[name] = t
        return t.ap()

    def sb(self, stack, name, shape, dt):
        return stack.enter_context(self.nc.sbuf_tensor(self.pg.uid(name), list(shape), dt))

    def dma(self, q, out, in_, reads, writes, **kw):
        def fn(e, out=out, in_=in_, kw=kw):
            return e.dma_start(out=out, in_=in_, **kw)
        return self.pg.add(q, fn, reads, writes, dma=True)

    def mm_group(self, out, pairs, reads, writes, **kw):
        def fn(e, out=out, pairs=pairs, kw=kw):
            n = len(pairs)
            ins = None
            for i, (l, r) in enumerate(pairs):
                ins = e.matmul(out, l, r, start=(i == 0), stop=(i == n - 1), **kw)
            return ins
        return self.pg.add("pe", fn, reads, writes)

    def act(self, out, in_, func, reads, writes, eng="act", **kw):
        def fn(e, out=out, in_=in_, func=func, kw=kw):
            return e.activation(out=out, in_=in_, func=func, **kw)
        return self.pg.add(eng, fn, reads, writes)

    def tt(self, out, in0, in1, op, reads, writes, eng="dve"):
        def fn(e, out=out, in0=in0, in1=in1, op=op):
            return e.tensor_tensor(out=out, in0=in0, in1=in1, op=op)
        return self.pg.add(eng, fn, reads, writes)

    def stt(self, out, in0, scalar, in1, op0, op1, reads, writes):
        def fn(e, out=out, in0=in0, scalar=scalar, in1=in1, op0=op0, op1=op1):
            return e.scalar_tensor_tensor(out=out, in0=in0, scalar=scalar, in1=in1, op0=op0, op1=op1)
        return self.pg.add("dve", fn, reads, writes)

    def ts(self, out, in0, s1, s2, op0, op1, reads, writes, eng="dve"):
        def fn(e, out=out, in0=in0, s1=s1, s2=s2, op0=op0, op1=op1):
            if op1 is None:
                return e.tensor_scalar(out=out, in0=in0, scalar1=s1, scalar2=None, op0=op0)
            return e.tensor_scalar(out=out, in0=in0, scalar1=s1, scalar2=s2, op0=op0, op1=op1)
        return self.pg.add(eng, fn, reads, writes)

    def copy(self, out, in_, reads, writes, eng="dve"):
        def fn(e, out=out, in_=in_):
            return e.tensor_copy(out=out, in_=in_)
        return self.pg.add(eng, fn, reads, writes)

    def memset(self, ap, val, writes, eng="dve"):
        def fn(e, ap=ap, val=val):
            return e.memset(ap, val)
        return self.pg.add(eng, fn, (), writes)

    def recip(self, out, in_, reads, writes):
        def fn(e, out=out, in_=in_):
            return e.reciprocal(out=out, in_=in_)
        return self.pg.add("dve", fn, reads, writes)

    def transpose(self, out, in_, ident, reads, writes):
        def fn(e, out=out, in_=in_, ident=ident):
            return e.transpose(out, in_, ident)
        return self.pg.add("pe", fn, reads, writes)

    def build(self):
        nc = self.nc
        pg = self.pg
        I = self.dram_in
        self.x_in = I("x", [NB, L, D], F32)
        self.ctx_in = I("ctx", [NB, C, D], F32)
        self.cT_in = I("cT", [P, KC, 3], F32)
        self.ngT_in = I("ngT", [P, DEPTH, 3, KC], F32)
        self.wmod_in = I("w_mod", [DEPTH, D, 9 * D], F32)
        self.bmodT_in = I("bmodT", [P, DEPTH, 72], F32)
        self.wgu_in = I("ffn_w_gu", [DEPTH, 2, D, 2 * DFF], F32)
        self.wdn_in = I("ffn_w_down", [DEPTH, 2, DFF, D], F32)
        self.ident_in = I("ident", [P, P], F32)
        self.win_ab_in = I("w_in_ab", [2, D, IN_EVEN], F32)
        self.wout_ab_in = I("w_out_ab", [2, D, D], F32)
        self.win_cd_in = I("w_in_cdp", [2, D, 1920], F32)
        self.wout_cd_in = I("w_out_cd", [2, D, D], F32)
        self.qkgA_in = I("qkgA", [P, 2, 2], F32)
        self.qkgD_in = I("qkgD", [P, 2, 2], F32)
        self.statm_in = I("statm", [P, 3, P], BF16)
        self.ctab_in = I("ctab", [P, 2, 256], BF16)
        self.strips_in = I("strips", [2, NA_HEADS, 64, 3584], F32)
        self.dftC = I("dftC", [L, L], BF16)
        self.dftS = I("dftS", [L, L], BF16)
        self.dft256 = I("dft256", [P, 2, 2, C], BF16)
        self.cosT_in = I("cosT", [P, L], F32)
        self.sinS_in = I("sinS", [P, L], F32)
        self.vgrow_in = I("vgrow", [P, 2, 512], F32)
        self.bsrow_in = I("bsrow", [P, 2, 4, 512], F32)
        self.wsT_in = I("wsT", [P, 2, 4, P], F32)
        self.out_ap = nc.dram_tensor("out", [NB, L, D], F32, kind="ExternalOutput").ap()
        self.XT = self.dram_scratch("XT", [D, NTOK], F32)
        self.QT = self.dram_scratch("QT", [NA_W, NTOK], BF16)
        self.KT = self.dram_scratch("KT", [NA_W, NTOK], BF16)
        self.V = self.dram_scratch("V", [NTOK, NA_W], BF16)
        self.U = self.dram_scratch("U", [NTOK, 512], BF16)
        self.MIXT = self.dram_scratch("MIXT", [D, NTOK], BF16)

        with contextlib.ExitStack() as gs:
            self.gs = gs
            self.psbig = gs.enter_context(nc.psum_tensor("psbig", [P, 8 * 512], F32))
            self.ps = [self.psbig[:, i * 512:(i + 1) * 512] for i in range(8)]
            self.ident = self.sb(gs, "ident", [P, P], F32)
            self.ones_bf = self.sb(gs, "ones", [P, P], BF16)
            self.silu_c = self.sb(gs, "siluc", [P, KC, 3], F32)
            self.ngT = self.sb(gs, "ngT", [P, DEPTH, 3, KC], F32)
            self.bmodT = self.sb(gs, "bmodT", [P, DEPTH, 72], F32)
            self.modraw2 = [self.sb(gs, "modraw", [P, 72, 3], F32) for _ in range(2)]
            self.modA2 = [self.sb(gs, "modA", [P, 3, KC, 3], F32) for _ in range(2)]
            self.modG2 = [self.sb(gs, "modG", [P, 3, KC, 3], F32) for _ in range(2)]
            self.modS2 = [self.sb(gs, "modS", [P, 3, KC, 3], F32) for _ in range(2)]
            self.bg = None
            self.dma("sp", self.ident[:], self.ident_in[:, :], (), ["ident"])
            self.dma("sp", self.silu_c[:], self.cT_in[:, :, :], (), ["siluc"])
            self.dma("sp", self.ngT[:], self.ngT_in[:, :, :, :], (), ["ngT"])
            self.dma("sp", self.bmodT[:], self.bmodT_in[:, :, :], (), ["bmodT"])
            self.memset(self.ones_bf[:], 1.0, ["ones"])
            self.ident_bf = self.sb(gs, "identbf", [P, P], BF16)
            self.statm = self.sb(gs, "statm", [P, 3, P], BF16)
            self.ctab = self.sb(gs, "ctab", [P, 2, 256], BF16)
            self.qkgA = self.sb(gs, "qkgA", [P, 2, 2], F32)
            self.qkgD = self.sb(gs, "qkgD", [P, 2, 2], F32)
            self.copy(self.ident_bf[:], self.ident[:], ["ident"], ["identbf"])
            self.dma("sp", self.statm[:], self.statm_in[:, :, :], (), ["statm"])
            self.dma("sp", self.ctab[:], self.ctab_in[:, :, :], (), ["ctab"])
            self.dma("sp", self.qkgA[:], self.qkgA_in[:, :, :], (), ["qkgA"])
            self.dma("sp", self.qkgD[:], self.qkgD_in[:, :, :], (), ["qkgD"])
            self.act(self.silu_c[:], self.silu_c[:], AF.Silu, ["siluc"], ["siluc"])

            self.phase_input()
            for l in range(self.nlayers):
                par = l % 2
                self.modA, self.modG, self.modS = self.modA2[par], self.modG2[par], self.modS2[par]
                self.kA, self.kG, self.kS = ("modA", par), ("modG", par), ("modS", par)
                self.phase_ffn(l, 0)
                if self.stages == "ffn0" and l == self.nlayers - 1:
                    break
                if l % 2 == 0:
                    self.phase_inproj_even(l)
                    if self.stages == "inproj" and l == self.nlayers - 1:
                        break
                    self.phase_na(l)
                    self.phase_fnet(l)
                else:
                    self.phase_inproj_odd(l)
                    if self.stages == "inproj" and l == self.nlayers - 1:
                        break
                    self.phase_gqa(l)
                if self.stages == "mix" and l == self.nlayers - 1:
                    break
                self.phase_outproj(l)
                if self.stages == "outproj" and l == self.nlayers - 1:
                    break
                self.phase_ffn(l, 1, ntiles=(NT if l < DEPTH - 1 else NTL))
            self.phase_output()
            pg.barrier()
            pg.emit(gs)
        return nc

    def phase_input(self):
        pg = self.pg
        with contextlib.ExitStack() as st:
            xin = [self.sb(st, "xin", [P, 4, D], F32) for _ in range(2)]
            xo = [self.sb(st, "xo", [P, KC, TN], F32) for _ in range(2)]
            self.bg = self.mod_task(0, st)
            for t in range(NT):
                i = t % 2
                self.bg_step()
                self.bg_step()
                if t < NTL:
                    b, t0 = divmod(t * TN, L)
                    src = self.x_in[b, t0:t0 + TN, :].rearrange("(s p) d -> p s d", p=P)
                    self.dma("sp", xin[i][:], src, (), [("xin", i, 0), ("xin", i, 1)])
                else:
                    for b in range(NB):
                        src = self.ctx_in[b, :, :].rearrange("(s p) d -> p s d", p=P)
                        self.dma("sp", xin[i][:, 2 * b:2 * b + 2, :], src, (), [("xin", i, b)])
                rk = [("xin", i, 0), ("xin", i, 1)]
                for k in range(KC):
                    bank = k % 4
                    for s in range(4):
                        self.transpose(self.ps[bank][:, s * P:(s + 1) * P], xin[i][:, s, k * P:(k + 1) * P],
                                       self.ident[:], rk + ["ident"], [("ps", bank)])
                    if k % 2 == 0:
                        self.copy(xo[i][:, k, :], self.ps[bank][:], [("ps", bank)], [("xo", i, k)], eng="dve")
                    else:
                        self.act(xo[i][:, k, :], self.ps[bank][:], AF.Copy, [("ps", bank)], [("xo", i, k)])
                dst = self.XT[:, t * TN:(t + 1) * TN].rearrange("(k p) n -> p k n", p=P)
                self.dma("sp", dst, xo[i][:], [("xo", i, k) for k in range(KC)],
                         [("XT", t, k) for k in range(KC)])
            self.bg_drain()
        pg.barrier()

    def phase_output(self):
        pg = self.pg
        with contextlib.ExitStack() as st:
            xi = [self.sb(st, "oxi", [P, KC, TN], F32) for _ in range(2)]
            xo = [self.sb(st, "oxo", [P, 4, D], F32) for _ in range(2)]
            for t in range(NTL):
                i = t % 2
                src = self.XT[:, t * TN:(t + 1) * TN].rearrange("(k p) n -> p k n", p=P)
                self.dma("sp", xi[i][:], src, [("XT", t, k) for k in range(KC)], [("oxi", i)])
                for s in range(4):
                    for kk in range(2):
                        bank = (s * 2 + kk) % 4
                        for k4 in range(4):
                            k = kk * 4 + k4
                            self.transpose(self.ps[bank][:, k4 * P:(k4 + 1) * P], xi[i][:, k, s * P:(s + 1) * P],
                                           self.ident[:], [("oxi", i), "ident"], [("ps", bank)])
                        if kk == 0:
                            self.copy(xo[i][:, s, kk * 512:(kk + 1) * 512], self.ps[bank][:], [("ps", bank)],
                                      [("oxo", i, s, kk)], eng="dve")
                        else:
                            self.act(xo[i][:, s, kk * 512:(kk + 1) * 512], self.ps[bank][:], AF.Copy,
                                     [("ps", bank)], [("oxo", i, s, kk)])
                b, t0 = divmod(t * TN, L)
                dst = self.out_ap[b, t0:t0 + TN, :].rearrange("(s p) d -> p s d", p=P)
                self.dma("sp", dst, xo[i][:], [("oxo", i, s, kk) for s in range(4) for kk in range(2)],
                         [("out", t)])
        pg.barrier()

    def bg_step(self):
        if self.bg is not None:
            try:
                next(self.bg)
            except StopIteration:
                self.bg = None

    def bg_drain(self):
        while self.bg is not None:
            self.bg_step()

    def mod_task(self, l, st):
        par = l % 2
        NBLK = 18
        wm = [self.sb(st, "wm", [P, KC, 512], F32) for _ in range(2)]
        modraw = self.modraw2[par]
        ps6 = self.ps[6]
        ps6v = ps6[:, 0:12].rearrange("p (j n) -> p j n", n=3)

        def load(jb):
            src = self.wmod_in[l, :, jb * 512:(jb + 1) * 512].rearrange("(k p) j -> p k j", p=P)
            self.dma("sp", wm[jb % 2][:], src, (), [("wm", l, jb % 2)])

        load(0)
        yield
        for jb in range(NBLK):
            if jb + 1 < NBLK:
                load(jb + 1)
            for j in range(4):
                pairs = [(wm[jb % 2][:, k, j * P:(j + 1) * P], self.silu_c[:, k, :]) for k in range(KC)]
                self.mm_group(ps6[:, j * 3:j * 3 + 3], pairs, [("wm", l, jb % 2), "siluc"], [("ps", 6)])
            for n in range(3):
                self.tt(modraw[:, jb * 4:jb * 4 + 4, n], ps6v[:, :, n], self.bmodT[:, l, jb * 4:jb * 4 + 4], ALU.add,
                        [("ps", 6), "bmodT"], [("modraw", par, jb, n)])
            yield
        allk = [("modraw", par, jb, n) for jb in range(NBLK) for n in range(3)]
        for s in range(3):
            for n in range(3):
                self.stt(self.modA2[par][:, s, :, n], modraw[:, s * 24 + 8:s * 24 + 16, n], 1.0,
                         self.ngT[:, l, s, :], ALU.add, ALU.mult, allk + ["ngT"], [("modA", par)])
            self.ts(self.modG2[par][:, s, :, :], modraw[:, s * 24 + 16:s * 24 + 24, :],
                    (1.0 if s == 1 else 0.5), None, ALU.mult, None, allk, [("modG", par)])
            self.copy(self.modS2[par][:, s, :, :], modraw[:, s * 24:s * 24 + 8, :], allk, [("modS", par)])
        yield

    def tile_n(self, t):
        return (t * TN) // L if t < NTL else 2

    def pre1(self, t, xa, hbuf, xk, hk):
        src = self.XT[:, t * TN:(t + 1) * TN].rearrange("(k p) n -> p k n", p=P)
        self.dma("sp", xa[:], src, [("XT", t, k) for k in range(KC)], xk)
        self.act(hbuf[:], xa[:], AF.Square, xk, hk)

    def pre2(self, t, s, xa, hbuf, xk, hk, rstd, rk, psbank):
        n = self.tile_n(t)
        pairs = [(self.ones_bf[:], hbuf[:, k, :]) for k in range(KC)]
        self.mm_group(self.ps[psbank][:], pairs, ["ones"] + hk, [("ps", psbank)])
        self.act(rstd[:], self.ps[psbank][:], AF.Ln, [("ps", psbank)], [rk], scale=1.0 / D, bias=EPS)
        self.act(rstd[:], rstd[:], AF.Exp, [rk], [rk], scale=-0.5)
        for k in range(KC):
            self.tt(xa[:, k, :], xa[:, k, :], rstd[:], ALU.mult, [xk[k], rk], [xk[k]])
            self.act(hbuf[:, k, :], xa[:, k, :], AF.Identity, [xk[k], self.kA, self.kS], [hk[k]],
                     scale=self.modA[:, s, k, n:n + 1], bias=self.modS[:, s, k, n:n + 1])

    def phase_ffn(self, l, which, ntiles=NT):
        pg = self.pg
        s = 0 if which == 0 else 2
        tag = "f%d_%d" % (l, which)
        with contextlib.ExitStack() as st:
            wgu = self.sb(st, "wgu", [P, KC, 2 * DFF], BF16)
            wdn = self.sb(st, "wdn", [P, FC, D], BF16)
            xa = self.sb(st, "xa", [P, KC, TN], F32)
            hb = [self.sb(st, "h", [P, KC, TN], BF16) for _ in range(2)]
            actb = self.sb(st, "actb", [P, FC, TN], BF16)
            xr = [self.sb(st, "xr", [P, TN], F32) for _ in range(2)]
            sg = [self.sb(st, "sg", [P, TN], F32) for _ in range(2)]
            rstd = self.sb(st, "rstd", [P, TN], F32)
            gu_src = self.wgu_in[l, which].rearrange("(k p) c -> p k c", p=P)
            for blk in range(FC // 2):
                for hh in range(2):
                    c0 = hh * DFF + blk * 256
                    self.dma("pool", wgu[:, :, c0:c0 + 256], gu_src[:, :, c0:c0 + 256], (), [(tag, "wgu", hh, blk)],
                             max_dma_last_dim=4096)
            for j in range(FC):
                src = self.wdn_in[l, which, j * P:(j + 1) * P, :]
                self.dma("pool", wdn[:, j, :], src, (), [(tag, "wdn", j)], max_dma_last_dim=4096)
            wdn_keys = [(tag, "wdn", j) for j in range(FC)]
            act_keys = [(tag, "act", j) for j in range(FC)]

            xk = [(tag, "xa", k) for k in range(KC)]
            hks = [[(tag, "h", ii, k) for k in range(KC)] for ii in range(2)]
            rk = (tag, "rstd")
            self.pre1(0, xa, hb[0], xk, hks[0])
            self.pre2(0, s, xa, hb[0], xk, hks[0], rstd, rk, 6)
            for t in range(ntiles):
                i = t % 2
                n = self.tile_n(t)
                hcur = hb[i]
                hk = hks[i]
                for j in range(FC):
                    pj = j % 2
                    bg, bu = 2 * pj, 2 * pj + 1
                    pairs = [(wgu[:, k, j * P:(j + 1) * P], hcur[:, k, :]) for k in range(KC)]
                    self.mm_group(self.ps[bg][:], pairs, [(tag, "wgu", 0, j // 2)] + hk, [("ps", bg)])
                    pairs = [(wgu[:, k, DFF + j * P:DFF + (j + 1) * P], hcur[:, k, :]) for k in range(KC)]
                    self.mm_group(self.ps[bu][:], pairs, [(tag, "wgu", 1, j // 2)] + hk, [("ps", bu)])
                    self.act(sg[pj][:], self.ps[bg][:], AF.Silu, [("ps", bg)], [(tag, "sg", pj)])
                    self.tt(actb[:, j, :], sg[pj][:], self.ps[bu][:], ALU.mult,
                            [(tag, "sg", pj), ("ps", bu)], [(tag, "act", j)])
                    if j == 15 and t + 1 < ntiles:
                        self.pre1(t + 1, xa, hb[1 - i], xk, hks[1 - i])
                if t + 1 < ntiles:
                    self.pre2(t + 1, s, xa, hb[1 - i], xk, hks[1 - i], rstd, rk, 6)
                for c in range(KC):
                    bd = 4 + (c % 2)
                    q = c % 2
                    self.dma("sp", xr[q][:], self.XT[c * P:(c + 1) * P, t * TN:(t + 1) * TN],
                             [("XT", t, c)], [(tag, "xr", q)])
                    pairs = [(wdn[:, j, c * P:(c + 1) * P], actb[:, j, :]) for j in range(FC)]
                    self.mm_group(self.ps[bd][:], pairs, wdn_keys + act_keys, [("ps", bd)])
                    self.stt(xr[q][:], self.ps[bd][:], self.modG[:, s, c, n:n + 1], xr[q][:], ALU.mult, ALU.add,
                             [("ps", bd), self.kG, (tag, "xr", q)], [(tag, "xr", q)])
                    self.dma("sp", self.XT[c * P:(c + 1) * P, t * TN:(t + 1) * TN], xr[q][:],
                             [(tag, "xr", q)], [("XT", t, c)])
        pg.barrier()


    def mm_multi(self, items, reads, writes):
        def fn(e, items=items):
            ins = None
            for (o, l, r, st_, sp_) in items:
                ins = e.matmul(o, l, r, start=st_, stop=sp_)
            return ins
        return self.pg.add("pe", fn, reads, writes)

    def qk_norm(self, psb, statm, statkey, gcol, gkey, sqb, sqk, rbuf, rkey, ssb, out_ap, out_key, n=TN,
                out_eng="dve"):
        ps = self.ps
        self.act(sqb[:, 0:n], ps[psb][:, 0:n], AF.Square, [("ps", psb)], [sqk])
        self.mm_group(ps[ssb][:, 0:n], [(statm, sqb[:, 0:n])], [statkey, sqk], [("ps", ssb)])
        self.act(rbuf[:, 0:n], ps[ssb][:, 0:n], AF.Ln, [("ps", ssb)], [rkey], scale=1.0 / HD, bias=EPS)
        self.act(rbuf[:, 0:n], rbuf[:, 0:n], AF.Exp, [rkey], [rkey], scale=-0.5)
        self.stt(out_ap, ps[psb][:, 0:n], gcol, rbuf[:, 0:n], ALU.mult, ALU.mult,
                 [("ps", psb), gkey, rkey], [out_key])

    def attention_stream(self, jobs, pbufs, tag):
        ps = self.ps
        SB = [0, 1, 2, 3]
        NS = len(SB)
        AHEAD = NS - 1
        nj = len(jobs)

        def emit_qk(i):
            jb = jobs[i]
            sbk = SB[i % NS]
            items = []
            nq = len(jb["qk"])
            for a_, (cols, l_, r_) in enumerate(jb["qk"]):
                lo, hi = cols
                items.append((ps[sbk][:, lo:hi], l_, r_, a_ == 0, a_ == nq - 1))
            self.mm_multi(items, jb["rq"], [("ps", sbk)])

        def emit_rest(i):
            jb = jobs[i]
            sbk = SB[i % NS]
            n = jb["n"]
            pb = pbufs[i % 3]
            pk = (tag, "pb", i % 3)
            self.act(pb[:, 0, 0:n], ps[sbk][:, 0:n], AF.Exp, [("ps", sbk)], [pk])
            acc = jb["acc"]
            self.mm_multi([(ps[acc][:, 0:n], jb["v"], pb[:, 0, 0:n], jb["first"], jb["last"])],
                          jb["rv"] + [pk], [("ps", acc)])
            if jb["last"] and jb["fin"] is not None:
                jb["fin"]()

        for i in range(min(AHEAD, nj)):
            emit_qk(i)
        for i in range(nj):
            if i + AHEAD < nj:
                emit_qk(i + AHEAD)
            emit_rest(i)
            if i % 48 == 0:
                self.bg_step()

    def att_finalize(self, acc, n, rden, rkey, out_ap, out_key):
        ps = self.ps
        self.recip(rden[:, 0:n], ps[acc][64:128, 0:n], [("ps", acc)], [rkey])
        self.tt(out_ap, ps[acc][0:64, 0:n], rden[:, 0:n], ALU.mult, [("ps", acc), rkey], [out_key])

    def run_units(self, units, mid_hook=None, main=(0, 1, 2), cell=None):
        n = len(units)
        if cell is None:
            cell = [0]
        bank_of = {}
        for k in range(n + 2):
            if k < n:
                if units[k][3]:
                    bank_of[k] = main[cell[0] % len(main)]
                    cell[0] += 1
                else:
                    bank_of[k] = None
                if units[k][0] is not None:
                    units[k][0](bank_of[k])
            if 0 <= k - 1 < n and units[k - 1][1] is not None:
                units[k - 1][1](bank_of[k - 1])
            if 0 <= k - 2 < n and units[k - 2][2] is not None:
                units[k - 2][2](bank_of[k - 2])
            if mid_hook is not None and k == n // 2:
                mid_hook()

    def qk_norm_a(self, psb, statm, statkey, sqb, sqk, ssb, n=TN):
        ps = self.ps
        self.act(sqb[:, 0:n], ps[psb][:, 0:n], AF.Square, [("ps", psb)], [sqk])
        self.mm_group(ps[ssb][:, 0:n], [(statm, sqb[:, 0:n])], [statkey, sqk], [("ps", ssb)])

    def qk_norm_b(self, psb, gcol, gkey, rbuf, rkey, ssb, out_ap, out_key, n=TN):
        ps = self.ps
        self.act(rbuf[:, 0:n], ps[ssb][:, 0:n], AF.Ln, [("ps", ssb)], [rkey], scale=1.0 / HD, bias=EPS)
        self.act(rbuf[:, 0:n], rbuf[:, 0:n], AF.Exp, [rkey], [rkey], scale=-0.5)
        self.stt(out_ap, ps[psb][:, 0:n], gcol, rbuf[:, 0:n], ALU.mult, ALU.mult,
                 [("ps", psb), gkey, rkey], [out_key])

    def phase_inproj_even(self, l):
        pg = self.pg
        i_ = l // 2
        s = 1
        tag = "ie%d" % l
        with_ctx = l < DEPTH - 1
        ps = self.ps
        MAIN = [0, 1, 2, 5]
        SSB = [3, 4]
        R = 3
        with contextlib.ExitStack() as st:
            win = self.sb(st, "win", [P, KC, IN_EVEN], BF16)
            xa = [self.sb(st, "xa", [P, KC, TN], F32) for _ in range(2)]
            hb = [self.sb(st, "h", [P, KC, TN], BF16) for _ in range(2)]
            rstd = self.sb(st, "rstd", [P, TN], F32)
            sqb = [self.sb(st, "sqb", [P, TN], BF16) for _ in range(R)]
            rb = [self.sb(st, "rb", [P, TN], F32) for _ in range(R)]
            qst = [self.sb(st, "qst", [P, 6, TN], BF16) for _ in range(2)]
            kst = [self.sb(st, "kst", [P, 6, TN], BF16) for _ in range(2)]
            vst = [self.sb(st, "vst", [P, 4, NA_W], BF16) for _ in range(2)]
            zst = [self.sb(st, "zst", [P, 2, TN], BF16) for _ in range(2)]
            ust = [self.sb(st, "ust", [P, 4, 512], BF16) for _ in range(2)]
            gq = self.sb(st, "gq", [P, 1], F32)
            w_src = self.win_ab_in[i_].rearrange("(k p) c -> p k c", p=P)
            order = [(3 * NA_W, 256, "z")] + [(c0, 256, "qk%d" % (c0 // 256)) for c0 in range(0, 2 * NA_W, 256)] + \
                    [(2 * NA_W + c0, 256, "v%d" % (c0 // 256)) for c0 in range(0, NA_W, 256)]
            wkey = {}
            for (c0, w, nm) in order:
                self.dma("pool", win[:, :, c0:c0 + w], w_src[:, :, c0:c0 + w], (), [(tag, "win", nm)],
                         max_dma_last_dim=4096)
                for cc in range(c0, c0 + w, P):
                    wkey[cc] = (tag, "win", nm)
            self.ts(gq[:], self.qkgA[:, i_, 0:1], 0.125, None, ALU.mult, None, ["qkgA"], [(tag, "gq")])
            xks = [[(tag, "xa", ii, k) for k in range(KC)] for ii in range(2)]
            hks = [[(tag, "h", ii, k) for k in range(KC)] for ii in range(2)]
            rk = (tag, "rstd")
            cnt = [0, 0, 0]
            bankcell = [0]
            self.pre1(0, xa[0], hb[0], xks[0], hks[0])
            self.pre2(0, s, xa[0], hb[0], xks[0], hks[0], rstd, rk, 6)
            for t in range(NT):
                i = t % 2
                isctx = t >= NTL
                if t + 1 < NT:
                    self.pre1(t + 1, xa[1 - i], hb[1 - i], xks[1 - i], hks[1 - i])
                h = hb[i]
                hk = hks[i]
                cols = slice(t * TN, (t + 1) * TN)
                units = []

                def feat_unit(c0, post, mid=None):
                    def pe(bank, c0=c0):
                        pairs = [(win[:, k, c0:c0 + P], h[:, k, :]) for k in range(KC)]
                        self.mm_group(ps[bank][:], pairs, [wkey[c0]] + hk, [("ps", bank)])
                    units.append((pe, mid, post, True))

                do_f = (not isctx) or with_ctx
                if do_f:
                    for fc in range(2):
                        def post(bank, fc=fc):
                            self.copy(zst[i][:, fc, :], ps[bank][:], [("ps", bank)], [(tag, "zst", i, fc)], eng="dve")
                        feat_unit(3 * NA_W + fc * P, post)
                for which in range(2):
                    if which == 0 and isctx and not with_ctx:
                        continue
                    stg = qst[i] if which == 0 else kst[i]
                    sname = "qst" if which == 0 else "kst"
                    for qc in range(6):
                        x = cnt[1] % R
                        ssb = SSB[cnt[1] % 2]
                        cnt[1] += 1

                        def mid(bank, x=x, ssb=ssb):
                            self.qk_norm_a(bank, self.statm[:, 0, :], "statm", sqb[x], (tag, "sqb", x), ssb)

                        def post(bank, which=which, qc=qc, stg=stg, sname=sname, x=x, ssb=ssb):
                            gcol = gq[:] if which == 0 else self.qkgA[:, i_, 1:2]
                            gkey = (tag, "gq") if which == 0 else "qkgA"
                            self.qk_norm_b(bank, gcol, gkey, rb[x], (tag, "rb", x), ssb, stg[:, qc, :],
                                           (tag, sname, i, qc))
                            if qc == 5:
                                dram = self.QT if which == 0 else self.KT
                                dname = "QT" if which == 0 else "KT"
                                dst = dram[0:NA_W, cols].rearrange("(c p) n -> p c n", p=P)
                                self.dma("sp", dst, stg[:], [(tag, sname, i, q_) for q_ in range(6)], [(dname, t)])
                        feat_unit(which * NA_W + qc * P, post, mid)
                for sub in range(4):
                    for half in range(2):
                        c0 = 2 * NA_W + half * 512
                        w = 512 if half == 0 else 256

                        def pe(bank, sub=sub, c0=c0, w=w):
                            pa = [(h[:, k, sub * P:(sub + 1) * P], win[:, k, c0:c0 + w]) for k in range(KC)]
                            self.mm_group(ps[bank][:, 0:w], pa, [wkey[c0], wkey[c0 + w - P]] + hk, [("ps", bank)])

                        def post(bank, sub=sub, half=half, w=w):
                            o = vst[i][:, sub, half * 512:half * 512 + w]
                            if half == 0:
                                self.copy(o, ps[bank][:, 0:w], [("ps", bank)], [(tag, "vst", i, sub, half)], eng="dve")
                            else:
                                self.act(o, ps[bank][:, 0:w], AF.Copy, [("ps", bank)], [(tag, "vst", i, sub, half)])
                            if sub == 3 and half == 1:
                                dst = self.V[t * TN:(t + 1) * TN, 0:NA_W].rearrange("(s p) f -> p s f", p=P)
                                self.dma("sp", dst, vst[i][:],
                                         [(tag, "vst", i, s_, h_) for s_ in range(4) for h_ in range(2)], [("V", t)])
                        units.append((pe, None, post, True))
                if do_f:
                    ctab = self.ctab[:, 1 if isctx else 0, :]
                    for sub in range(4):
                        def pe(bank, sub=sub):
                            items = []
                            for fc in range(2):
                                items.append((ps[bank][:, fc * 256:(fc + 1) * 256], zst[i][:, fc, sub * P:(sub + 1) * P],
                                              ctab, True, True))
                            self.mm_multi(items, [(tag, "zst", i, 0), (tag, "zst", i, 1), "ctab"], [("ps", bank)])

                        def post(bank, sub=sub):
                            if sub % 2 == 0:
                                self.copy(ust[i][:, sub, :], ps[bank][:], [("ps", bank)], [(tag, "ust", i, sub)], eng="dve")
                            else:
                                self.act(ust[i][:, sub, :], ps[bank][:], AF.Copy, [("ps", bank)], [(tag, "ust", i, sub)])
                            if sub == 3:
                                dst = self.U[t * TN:(t + 1) * TN, :].rearrange("(s p) f -> p s f", p=P)
                                self.dma("sp", dst, ust[i][:], [(tag, "ust", i, s_) for s_ in range(4)], [("U", t)])
                        units.append((pe, None, post, True))

                def mid(t=t, i=i):
                    if t + 1 < NT:
                        self.pre2(t + 1, s, xa[1 - i], hb[1 - i], xks[1 - i], hks[1 - i], rstd, rk, 6)
                self.run_units(units, mid, main=MAIN, cell=bankcell)
        pg.barrier()

    def na_bias_items(self, g, c):
        kr = 2 * c
        MID, A = 0, 1
        if g == 0:
            first = (0, 256, A, 15 - kr) if c <= 3 else (0, 256, MID, 0)
            return [first, (256, 512, MID, 15 - kr)]
        if g == 7:
            second = (320, 512, A, 76 - kr) if c >= 28 else (320, 512, MID, 0)
            return [(0, 320, MID, 67 - kr), second]
        return [(0, 512, MID, 11 - kr + 8 * g)]

    def phase_na(self, l):
        pg = self.pg
        i_ = l // 2
        tag = "na%d" % l
        with_ctx = l < DEPTH - 1
        ps = self.ps
        MIDW, AW = 1664, 2048
        NTK = L + C
        with contextlib.ExitStack() as st:
            qz = [self.sb(st, "qz", [P, 2, NTK], BF16) for _ in range(2)]
            ksb = [self.sb(st, "ksb", [P, NTK], BF16) for _ in range(2)]
            vaug = [self.sb(st, "vaug", [P, 34, 2, P], BF16) for _ in range(2)]
            bt = [self.sb(st, "bt", [P, 2, MIDW + AW], BF16) for _ in range(2)]
            ost = [self.sb(st, "ost", [P, NTK], BF16) for _ in range(2)]
            pbufs = [self.sb(st, "pb", [P, 2, TN], BF16) for _ in range(3)]
            rden = [self.sb(st, "rden", [64, TN], F32) for _ in range(2)]
            if l + 1 < self.nlayers:
                self.bg = self.mod_task(l + 1, st)
            for i in range(2):
                self.memset(qz[i][:], 0.0, [(tag, "qz", i, hh, x) for hh in range(2) for x in range(2)], eng="pool")
                self.memset(vaug[i][:], 1.0, [(tag, "vaug", i, hh, x) for hh in range(2) for x in range(2)], eng="pool")
                self.memset(bt[i][:], -1e30, [(tag, "bt", i, hh, x) for hh in range(2) for x in range(4)], eng="pool")
            it = 0
            fcount = [0]
            for b in range(NB):
                lat0 = b * L
                cx0 = NLAT + b * C
                for hp in range(6):
                    i = it % 2
                    it += 1
                    for hh in range(2):
                        r0 = hp * P + hh * 64
                        self.dma("sp", qz[i][hh * 64:hh * 64 + 64, hh, 0:L], self.QT[r0:r0 + 64, lat0:lat0 + L],
                                 [("QT", t) for t in range(b * 8, b * 8 + 8)], [(tag, "qz", i, hh, 0)])
                        if with_ctx:
                            self.dma("sp", qz[i][hh * 64:hh * 64 + 64, hh, L:NTK], self.QT[r0:r0 + 64, cx0:cx0 + C],
                                     [("QT", NTL)], [(tag, "qz", i, hh, 1)])
                    self.dma("sp", ksb[i][:, 0:L], self.KT[hp * P:(hp + 1) * P, lat0:lat0 + L],
                             [("KT", t) for t in range(b * 8, b * 8 + 8)], [(tag, "ksb", i, 0)])
                    self.dma("sp", ksb[i][:, L:NTK], self.KT[hp * P:(hp + 1) * P, cx0:cx0 + C],
                             [("KT", NTL)], [(tag, "ksb", i, 1)])
                    for hh in range(2):
                        f0 = hp * P + hh * 64
                        self.dma("sp", vaug[i][:, 0:32, hh, 0:64],
                                 self.V[lat0:lat0 + L, f0:f0 + 64].rearrange("(c p) d -> p c d", p=P),
                                 [("V", t) for t in range(b * 8, b * 8 + 8)], [(tag, "vaug", i, hh, 0)])
                        self.dma("sp", vaug[i][:, 32:34, hh, 0:64],
                                 self.V[cx0:cx0 + C, f0:f0 + 64].rearrange("(c p) d -> p c d", p=P),
                                 [("V", NTL)], [(tag, "vaug", i, hh, 1)])
                        hd = hp * 2 + hh
                        self.dma("pool", bt[i][0:64, hh, 0:1600], self.strips_in[i_, hd, :, 0:1600], (),
                                 [(tag, "bt", i, hh, 0)], max_dma_last_dim=4096)
                        self.dma("pool", bt[i][64:128, hh, 64:1664], self.strips_in[i_, hd, :, 0:1600], (),
                                 [(tag, "bt", i, hh, 1)], max_dma_last_dim=4096)
                        self.dma("pool", bt[i][0:64, hh, MIDW:MIDW + 1984], self.strips_in[i_, hd, :, 1600:3584], (),
                                 [(tag, "bt", i, hh, 2)], max_dma_last_dim=4096)
                        self.dma("pool", bt[i][64:128, hh, MIDW + 64:MIDW + 2048], self.strips_in[i_, hd, :, 1600:3584], (),
                                 [(tag, "bt", i, hh, 3)], max_dma_last_dim=4096)
                    jobs = []
                    for hh in range(2):
                        for g in range(9 if with_ctx else 8):
                            acc = 4 + fcount[0] % 2
                            rd = rden[fcount[0] % 2]
                            rdk = (tag, "rden", fcount[0] % 2)
                            fcount[0] += 1
                            if g < 8:
                                n = TN
                                q0 = g * TN
                                chunks = [c for c in range(4 * g - 2, 4 * g + 6) if 0 <= c < 32] + [32, 33]
                            else:
                                n = C
                                q0 = L
                                chunks = [32, 33]
                            qrhs = qz[i][:, hh, q0:q0 + n]
                            okey = (tag, "ost", i, hh, g)

                            def fin(acc=acc, n=n, rd=rd, rdk=rdk, i=i, hh=hh, q0=q0, okey=okey):
                                self.att_finalize(acc, n, rd, rdk, ost[i][hh * 64:hh * 64 + 64, q0:q0 + n], okey)

                            for ci, c in enumerate(chunks):
                                qk = [((0, n), ksb[i][:, c * P:(c + 1) * P], qrhs)]
                                if c < 32 and g < 8:
                                    for (lo, hi, strip, blk) in self.na_bias_items(g, c):
                                        off = (0 if strip == 0 else MIDW) + blk * 64
                                        qk.append(((lo, hi), self.ident_bf[:], bt[i][:, hh, off:off + (hi - lo)]))
                                jobs.append(dict(qk=qk, n=n, v=vaug[i][:, c, hh, :], first=(ci == 0),
                                                 last=(ci == len(chunks) - 1), acc=acc,
                                                 rq=[(tag, "qz", i, hh, 0), (tag, "qz", i, hh, 1), (tag, "ksb", i, 0), (tag, "ksb", i, 1), "identbf"]
                                                 + [(tag, "bt", i, hh, x) for x in range(4)],
                                                 rv=[(tag, "vaug", i, hh, 0), (tag, "vaug", i, hh, 1)], fin=fin))
                    self.attention_stream(jobs, pbufs, tag)
                    okeys = [(tag, "ost", i, hh, g) for hh in range(2) for g in range(9 if with_ctx else 8)]
                    self.dma("sp", self.MIXT[hp * P:(hp + 1) * P, lat0:lat0 + L], ost[i][:, 0:L], okeys,
                             [("MIXT", "na", b, hp)])
                    if with_ctx:
                        self.dma("sp", self.MIXT[hp * P:(hp + 1) * P, cx0:cx0 + C], ost[i][:, L:NTK], okeys,
                                 [("MIXT", "nac", b, hp)])
            self.bg_drain()
        pg.barrier()

    def phase_fnet(self, l):
        pg = self.pg
        tag = "fn%d" % l
        with_ctx = l < DEPTH - 1
        ps = self.ps
        with contextlib.ExitStack() as st:
            usb = [self.sb(st, "usb", [P, 32, 512], BF16) for _ in range(NB)]
            tb = [self.sb(st, "tb", [P, 32, TN], BF16) for _ in range(2)]
            yst = [self.sb(st, "yst", [P, 4, TN], BF16) for _ in range(2)]
            for b in range(NB):
                src = self.U[b * L:(b + 1) * L, :].rearrange("(c p) f -> p c f", p=P)
                self.dma("sp", usb[b][:], src, [("U", t) for t in range(b * 8, b * 8 + 8)], [(tag, "usb", b)])
            steps = [(j, part) for j in range(8) for part in range(2)]

            def load(si):
                j, part = steps[si]
                tab = self.dftC if part == 0 else self.dftS
                src = tab[:, j * TN:(j + 1) * TN].rearrange("(c p) n -> p c n", p=P)
                self.dma("sp", tb[si % 2][:], src, (), [(tag, "tb", si % 2)])

            load(0)
            for si, (j, part) in enumerate(steps):
                bset = (j % 2) * 4
                tbi = si % 2
                if si + 1 < len(steps):
                    load(si + 1)
                for c in range(32):
                    items = []
                    for b in range(NB):
                        for fc in range(2):
                            o0 = fc * 256 + part * P
                            items.append((ps[bset + b * 2 + fc][:], usb[b][:, c, o0:o0 + P], tb[tbi][:, c, :],
                                          part == 0 and c == 0, part == 1 and c == 31))
                    self.mm_multi(items, [(tag, "usb", 0), (tag, "usb", 1), (tag, "tb", tbi)],
                                  [("ps", bset + q) for q in range(4)])
                if part == 0:
                    continue
                yi = j % 2
                for q in range(4):
                    if q % 2 == 0:
                        self.copy(yst[yi][:, q, :], ps[bset + q][:], [("ps", bset + q)], [(tag, "yst", yi, q)], eng="dve")
                    else:
                        self.act(yst[yi][:, q, :], ps[bset + q][:], AF.Copy, [("ps", bset + q)], [(tag, "yst", yi, q)])
                for b in range(NB):
                    dst = self.MIXT[NA_W:D, b * L + j * TN:b * L + (j + 1) * TN].rearrange("(f p) n -> p f n", p=P)
                    self.dma("sp", dst, yst[yi][:, 2 * b:2 * b + 2, :], [(tag, "yst", yi, 2 * b), (tag, "yst", yi, 2 * b + 1)],
                             [("MIXT", "fn", b, j)])
            if with_ctx:
                usc = self.sb(st, "usc", [P, NB, 2, 512], BF16)
                tc_ = self.sb(st, "tc", [P, 2, 2, C], BF16)
                ysc = self.sb(st, "ysc", [P, NB, 2, C], BF16)
                for b in range(NB):
                    src = self.U[NLAT + b * C:NLAT + (b + 1) * C, :].rearrange("(c p) f -> p c f", p=P)
                    self.dma("sp", usc[:, b, :, :], src, [("U", NTL)], [(tag, "usc")])
                self.dma("sp", tc_[:], self.dft256[:, :, :, :], (), [(tag, "tc")])
                for b in range(NB):
                    for fc in range(2):
                        bank = b * 2 + fc
                        items = []
                        for part in range(2):
                            for c in range(2):
                                o0 = fc * 256 + part * P
                                items.append((ps[bank][:, 0:C], usc[:, b, c, o0:o0 + P], tc_[:, part, c, :],
                                              part == 0 and c == 0, part == 1 and c == 1))
                        self.mm_multi(items, [(tag, "usc"), (tag, "tc")], [("ps", bank)])
                        self.copy(ysc[:, b, fc, :], ps[bank][:, 0:C], [("ps", bank)], [(tag, "ysc", b, fc)], eng="dve")
                    dst = self.MIXT[NA_W:D, NLAT + b * C:NLAT + (b + 1) * C].rearrange("(f p) n -> p f n", p=P)
                    self.dma("sp", dst, ysc[:, b, :, :], [(tag, "ysc", b, 0), (tag, "ysc", b, 1)], [("MIXT", "fnc", b)])
        pg.barrier()

    def phase_outproj(self, l):
        pg = self.pg
        tag = "op%d" % l
        s = 1
        ps = self.ps
        ntiles = NT if l < DEPTH - 1 else NTL
        w_dram = self.wout_ab_in if l % 2 == 0 else self.wout_cd_in
        i_ = l // 2
        with contextlib.ExitStack() as st:
            wout = self.sb(st, "wout", [P, KC, D], BF16)
            mt = [self.sb(st, "mt", [P, KC, TN], BF16) for _ in range(2)]
            xr = [[self.sb(st, "xr", [P, TN], F32) for _ in range(KC)] for _ in range(2)]
            for k in range(KC):
                self.dma("pool", wout[:, k, :], w_dram[i_, k * P:(k + 1) * P, :], (), [(tag, "w", k)],
                         max_dma_last_dim=4096)
            wk = [(tag, "w", k) for k in range(KC)]

            def loads(t):
                i = t % 2
                src = self.MIXT[:, t * TN:(t + 1) * TN].rearrange("(k p) n -> p k n", p=P)
                self.dma("sp", mt[i][:], src, [], [(tag, "mt", i)])
                for c in range(KC):
                    self.dma("sp", xr[i][c][:], self.XT[c * P:(c + 1) * P, t * TN:(t + 1) * TN],
                             [("XT", t, c)], [(tag, "xr", i, c)])

            loads(0)
            for t in range(ntiles):
                i = t % 2
                n = self.tile_n(t)
                if t + 1 < ntiles:
                    loads(t + 1)
                for c in range(KC):
                    bd = c % 4
                    pairs = [(wout[:, k, c * P:(c + 1) * P], mt[i][:, k, :]) for k in range(KC)]
                    self.mm_group(ps[bd][:], pairs, wk + [(tag, "mt", i)], [("ps", bd)])
                    self.stt(xr[i][c][:], ps[bd][:], self.modG[:, s, c, n:n + 1], xr[i][c][:], ALU.mult, ALU.add,
                             [("ps", bd), self.kG, (tag, "xr", i, c)], [(tag, "xr", i, c)])
                    self.dma("act", self.XT[c * P:(c + 1) * P, t * TN:(t + 1) * TN], xr[i][c][:],
                             [(tag, "xr", i, c)], [("XT", t, c)])
        pg.barrier()

    def phase_inproj_odd(self, l):
        pg = self.pg
        i_ = l // 2
        s = 1
        tag = "io%d" % l
        with_ctx = l < DEPTH - 1
        ps = self.ps
        WQ, WK, WV, WU, WS = 0, 512, 768, 896, 1408
        GELU = AF.Gelu_apprx_tanh
        MAIN = [0, 1, 2]
        SSB = [3, 4]
        MIXB = [5, 7, 6]
        R = 3
        with contextlib.ExitStack() as st:
            win = self.sb(st, "win", [P, KC, 1920], BF16)
            xa = [self.sb(st, "xa", [P, KC, TN], F32) for _ in range(2)]
            hb = [self.sb(st, "h", [P, KC, TN], BF16) for _ in range(2)]
            rstd = self.sb(st, "rstd", [P, TN], F32)
            sqb = [self.sb(st, "sqb", [P, TN], BF16) for _ in range(R)]
            rb = [self.sb(st, "rb", [P, TN], F32) for _ in range(R)]
            qn = [self.sb(st, "qn", [P, TN], F32) for _ in range(R)]
            t1 = [self.sb(st, "t1", [P, TN], F32) for _ in range(R)]
            t2 = [self.sb(st, "t2", [P, TN], F32) for _ in range(R)]
            qst = [self.sb(st, "qst", [P, 4, TN], BF16) for _ in range(2)]
            kst = [self.sb(st, "kst", [P, 2, TN], BF16) for _ in range(2)]
            vst = [self.sb(st, "vst", [P, 4, P], BF16) for _ in range(2)]
            ug = self.sb(st, "ug", [P, 4, TN], F32)
            gs = [self.sb(st, "gs", [P, 512], F32) for _ in range(4)]
            junk = self.sb(st, "junk", [P, P], BF16)
            ssq = [self.sb(st, "ssq", [P, 16], F32) for _ in range(2)]
            vn = [self.sb(st, "vn", [P, 512], BF16) for _ in range(4)]
            mtmp = [self.sb(st, "mtmp", [P, 4, P], F32) for _ in range(2)]
            mixst = [self.sb(st, "mixst", [P, 4, TN], BF16) for _ in range(2)]
            cosT = self.sb(st, "cosT", [P, L], F32)
            sinS = self.sb(st, "sinS", [P, L], F32)
            vgrow = self.sb(st, "vgrow", [P, 512], F32)
            bsrow = self.sb(st, "bsrow", [P, 4, P], F32)
            wsT = self.sb(st, "wsT", [P, 4, P], BF16)
            gq = self.sb(st, "gq", [P, 1], F32)
            w_src = self.win_cd_in[i_].rearrange("(k p) c -> p k c", p=P)
            wkey = {}
            for c0 in range(0, 1920, 256):
                w = min(256, 1920 - c0)
                self.dma("pool", win[:, :, c0:c0 + w], w_src[:, :, c0:c0 + w], (), [(tag, "win", c0)],
                         max_dma_last_dim=4096)
                for cc in range(c0, c0 + w, P):
                    wkey[cc] = (tag, "win", c0)
            self.dma("pool", wsT[:], self.wsT_in[:, i_, :, :], (), [(tag, "wsT")])
            self.dma("sp", cosT[:], self.cosT_in[:, :], (), [(tag, "cosT")])
            self.dma("sp", sinS[:], self.sinS_in[:, :], (), [(tag, "sinS")])
            self.dma("sp", vgrow[:], self.vgrow_in[:, i_, :], (), [(tag, "vgrow")])
            self.dma("sp", bsrow[:], self.bsrow_in[:, i_, :, 0:P], (), [(tag, "bsrow")])
            self.ts(gq[:], self.qkgD[:, i_, 0:1], 0.125, None, ALU.mult, None, ["qkgD"], [(tag, "gq")])
            xks = [[(tag, "xa", ii, k) for k in range(KC)] for ii in range(2)]
            hks = [[(tag, "h", ii, k) for k in range(KC)] for ii in range(2)]
            rk = (tag, "rstd")
            cnt = [0, 0, 0, 0]
            bankcell = [0]
            self.pre1(0, xa[0], hb[0], xks[0], hks[0])
            self.pre2(0, s, xa[0], hb[0], xks[0], hks[0], rstd, rk, 6)
            for t in range(NT):
                i = t % 2
                isctx = t >= NTL
                if t + 1 < NT:
                    self.pre1(t + 1, xa[1 - i], hb[1 - i], xks[1 - i], hks[1 - i])
                h = hb[i]
                hk = hks[i]
                cols = slice(t * TN, (t + 1) * TN)
                p0 = (t * TN) % L
                units = []

                def feat_unit(c0, post, mid=None):
                    def pe(bank, c0=c0):
                        pairs = [(win[:, k, c0:c0 + P], h[:, k, :]) for k in range(KC)]
                        self.mm_group(ps[bank][:], pairs, [wkey[c0]] + hk, [("ps", bank)])
                    units.append((pe, mid, post, True))

                do_sgu = (not isctx) or with_ctx
                if do_sgu:
                    for g in range(4):
                        def post(bank, g=g):
                            self.act(ug[:, g, :], ps[bank][:], GELU, [("ps", bank)], [(tag, "ug", g)])
                        feat_unit(WU + g * P, post)
                n_u = len(units)
                for which in range(2):
                    if which == 0 and isctx and not with_ctx:
                        continue
                    nch = 4 if which == 0 else 2
                    stg = qst[i] if which == 0 else kst[i]
                    sname = "qst" if which == 0 else "kst"
                    for qc in range(nch):
                        x = cnt[1] % R
                        ssb = SSB[cnt[1] % 2]
                        cnt[1] += 1

                        def mid(bank, which=which, x=x, ssb=ssb):
                            sm = self.statm[:, 1, :] if which == 0 else self.statm[:, 2, :]
                            self.qk_norm_a(bank, sm, "statm", sqb[x], (tag, "sqb", x), ssb)

                        def post(bank, which=which, qc=qc, stg=stg, sname=sname, nch=nch, x=x, ssb=ssb):
                            gcol = gq[:] if which == 0 else self.qkgD[:, i_, 1:2]
                            gkey = (tag, "gq") if which == 0 else "qkgD"
                            okey = (tag, sname, i, qc)
                            if isctx:
                                self.qk_norm_b(bank, gcol, gkey, rb[x], (tag, "rb", x), ssb, stg[:, qc, :], okey)
                            else:
                                qk_ = (tag, "qn", x)
                                self.qk_norm_b(bank, gcol, gkey, rb[x], (tag, "rb", x), ssb, qn[x][:], qk_)
                                self.tt(t1[x][:], qn[x][:], cosT[:, p0:p0 + TN], ALU.mult, [qk_, (tag, "cosT")],
                                        [(tag, "t1", x)], eng="pool")
                                self.tt(t2[x][0:64, :], qn[x][64:128, :], sinS[64:128, p0:p0 + TN], ALU.mult,
                                        [qk_, (tag, "sinS")], [(tag, "t2", x, 0)], eng="pool")
                                self.tt(t2[x][64:128, :], qn[x][0:64, :], sinS[0:64, p0:p0 + TN], ALU.mult,
                                        [qk_, (tag, "sinS")], [(tag, "t2", x, 1)], eng="dve")
                                self.tt(stg[:, qc, :], t1[x][:], t2[x][:], ALU.add,
                                        [(tag, "t1", x), (tag, "t2", x, 0), (tag, "t2", x, 1)], [okey])
                            if qc == nch - 1:
                                dram = self.QT if which == 0 else self.KT
                                dname = "QT" if which == 0 else "KT"
                                dst = dram[0:nch * P, cols].rearrange("(c p) n -> p c n", p=P)
                                self.dma("sp", dst, stg[:], [(tag, sname, i, q_) for q_ in range(nch)], [(dname, t)])
                        feat_unit((WQ if which == 0 else WK) + qc * P, post, mid)
                def pe_v(bank):
                    for sub in range(4):
                        pa = [(h[:, k, sub * P:(sub + 1) * P], win[:, k, WV:WV + P]) for k in range(KC)]
                        self.mm_group(ps[bank][:, sub * P:(sub + 1) * P], pa, [wkey[WV]] + hk, [("ps", bank)])

                def post_v(bank):
                    self.copy(vst[i][:].rearrange("p s f -> p (s f)"), ps[bank][:], [("ps", bank)], [(tag, "vst", i)], eng="dve")
                    dst = self.V[t * TN:(t + 1) * TN, 0:P].rearrange("(s p) f -> p s f", p=P)
                    self.dma("sp", dst, vst[i][:], [(tag, "vst", i)], [("V", t)])
                units.append((pe_v, None, post_v, True))
                if do_sgu:
                    s_units = []
                    mix_units = []
                    xs_ = []
                    for sub in range(4):
                        x = cnt[2] % 4
                        cnt[2] += 1
                        xs_.append(x)

                        def pe(bank, sub=sub):
                            pa = [(h[:, k, sub * P:(sub + 1) * P], win[:, k, WS:WS + 512]) for k in range(KC)]
                            self.mm_group(ps[bank][:], pa, [wkey[WS], wkey[WS + 256]] + hk, [("ps", bank)])

                        def post(bank, x=x, sub=sub):
                            self.act(gs[x][:], ps[bank][:], GELU, [("ps", bank)], [(tag, "gs", x)])
                            for g in range(4):
                                def fn(e, o=junk[:], a=gs[x][:, g * P:(g + 1) * P],
                                       acc=ssq[i][:, sub * 4 + g:sub * 4 + g + 1]):
                                    return e.activation(out=o, in_=a, func=AF.Square, accum_out=acc)
                                pg.add("act", fn, [(tag, "gs", x)], [(tag, "ssq", i, sub, g)])
                        s_units.append((pe, None, post, True))
                        mb = MIXB[cnt[3] % 3]
                        mi = cnt[3] % 2
                        cnt[3] += 1

                        def pe_m(bank, mb=mb, x=x):
                            items = [(ps[mb][:, g * P:(g + 1) * P], vn[x][:, g * P:(g + 1) * P], wsT[:, g, :], True, True)
                                     for g in range(4)]
                            self.mm_multi(items, [(tag, "vn", x, g) for g in range(4)] + [(tag, "wsT")], [("ps", mb)])

                        def post_m(bank, mb=mb, mi=mi, sub=sub):
                            self.tt(mtmp[mi][:].rearrange("p g n -> p (g n)"), ps[mb][:], bsrow[:].rearrange("p g n -> p (g n)"),
                                    ALU.add, [("ps", mb), (tag, "bsrow")], [(tag, "mtmp", mi)])
                            self.tt(mixst[i][:, :, sub * P:(sub + 1) * P], mtmp[mi][:], ug[:, :, sub * P:(sub + 1) * P], ALU.mult,
                                    [(tag, "mtmp", mi)] + [(tag, "ug", g) for g in range(4)], [(tag, "mixst", i, sub)])
                            if sub == 3:
                                dst = self.MIXT[512:D, cols].rearrange("(g p) n -> p g n", p=P)
                                self.dma("sp", dst, mixst[i][:], [(tag, "mixst", i, s_) for s_ in range(4)],
                                         [("MIXT", "sgu", t)])
                        mix_units.append((pe_m, None, post_m, False))

                    def post_fin(bank, xs_=xs_):
                        sk = [(tag, "ssq", i, sub, g) for sub in range(4) for g in range(4)]
                        self.act(ssq[i][:], ssq[i][:], AF.Ln, sk, sk, scale=1.0 / P, bias=EPS)
                        self.act(ssq[i][:], ssq[i][:], AF.Exp, sk, sk, scale=-0.5)
                        for sub in range(4):
                            x = xs_[sub]
                            for g in range(4):
                                self.stt(vn[x][:, g * P:(g + 1) * P], gs[x][:, g * P:(g + 1) * P],
                                         ssq[i][:, sub * 4 + g:sub * 4 + g + 1],
                                         vgrow[:, g * P:(g + 1) * P], ALU.mult, ALU.mult,
                                         [(tag, "gs", x), (tag, "vgrow")] + sk, [(tag, "vn", x, g)])
                    sgu_front = s_units + [(None, None, post_fin, False)]
                    sgu_back = mix_units
                else:
                    sgu_front, sgu_back = [], []
                units = units[:n_u] + sgu_front + units[n_u:] + sgu_back

                def mid(t=t, i=i):
                    if t + 1 < NT:
                        self.pre2(t + 1, s, xa[1 - i], hb[1 - i], xks[1 - i], hks[1 - i], rstd, rk, 6)
                self.run_units(units, mid, main=MAIN, cell=bankcell)
        pg.barrier()

    def phase_gqa(self, l):
        pg = self.pg
        tag = "gq%d" % l
        with_ctx = l < DEPTH - 1
        ps = self.ps
        NTK = L + C
        with contextlib.ExitStack() as st:
            qz = [self.sb(st, "qz", [P, 4, NTK], BF16) for _ in range(2)]
            ksb = [self.sb(st, "ksb", [P, NTK], BF16) for _ in range(2)]
            vaug = [self.sb(st, "vaug", [P, 34, P], BF16) for _ in range(2)]
            ost = [self.sb(st, "ost", [P, 2, NTK], BF16) for _ in range(2)]
            pbufs = [self.sb(st, "pb", [P, 2, TN], BF16) for _ in range(3)]
            rden = [self.sb(st, "rden", [64, TN], F32) for _ in range(2)]
            if l + 1 < self.nlayers:
                self.bg = self.mod_task(l + 1, st)
            for i in range(2):
                self.memset(qz[i][:], 0.0, [(tag, "qz", i, q, x) for q in range(4) for x in range(4)], eng="pool")
                self.memset(vaug[i][:], 1.0, [(tag, "vaug", i, 0), (tag, "vaug", i, 1)], eng="pool")
            it = 0
            fcount = 0
            for b in range(NB):
                lat0 = b * L
                cx0 = NLAT + b * C
                for kv in range(2):
                    i = it % 2
                    it += 1
                    latk = [("QT", t) for t in range(b * 8, b * 8 + 8)]
                    self.dma("sp", ksb[i][:, 0:L], self.KT[kv * P:(kv + 1) * P, lat0:lat0 + L], [], [(tag, "ksb", i, 0)])
                    self.dma("sp", ksb[i][:, L:NTK], self.KT[kv * P:(kv + 1) * P, cx0:cx0 + C], [], [(tag, "ksb", i, 1)])
                    self.dma("sp", vaug[i][:, 0:32, 0:64],
                             self.V[lat0:lat0 + L, kv * 64:kv * 64 + 64].rearrange("(c p) d -> p c d", p=P),
                             [], [(tag, "vaug", i, 0)])
                    self.dma("sp", vaug[i][:, 32:34, 0:64],
                             self.V[cx0:cx0 + C, kv * 64:kv * 64 + 64].rearrange("(c p) d -> p c d", p=P),
                             [], [(tag, "vaug", i, 1)])
                    for qi in range(4):
                        qc = 2 * kv + qi // 2
                        hh = qi % 2
                        for half in range(2):
                            r0 = half * 64 + hh * 32
                            self.dma("sp", qz[i][r0:r0 + 32, qi, 0:L], self.QT[qc * P + r0:qc * P + r0 + 32, lat0:lat0 + L],
                                     [], [(tag, "qz", i, qi, half)])
                            if with_ctx:
                                self.dma("sp", qz[i][r0:r0 + 32, qi, L:NTK],
                                         self.QT[qc * P + r0:qc * P + r0 + 32, cx0:cx0 + C], [], [(tag, "qz", i, qi, 2 + half)])
                    jobs = []
                    okeys = []
                    for qi in range(4):
                        hh = qi % 2
                        ql = qi // 2
                        for g in range(9 if with_ctx else 8):
                            acc = 4 + fcount % 2
                            rd = rden[fcount % 2]
                            rdk = (tag, "rden", fcount % 2)
                            fcount += 1
                            if g < 8:
                                n, q0 = TN, g * TN
                                chunks = list(range(34))
                            else:
                                n, q0 = C, L
                                chunks = [32, 33]
                            qrhs = qz[i][:, qi, q0:q0 + n]
                            okey = (tag, "ost", i, qi, g)
                            okeys.append(okey)

                            def fin(acc=acc, n=n, rd=rd, rdk=rdk, i=i, hh=hh, ql=ql, q0=q0, okey=okey):
                                self.att_finalize(acc, n, rd, rdk, ost[i][hh * 64:hh * 64 + 64, ql, q0:q0 + n], okey)

                            for ci, c in enumerate(chunks):
                                qk = [((0, n), ksb[i][:, c * P:(c + 1) * P], qrhs)]
                                jobs.append(dict(qk=qk, n=n, v=vaug[i][:, c, :], first=(ci == 0),
                                                 last=(ci == len(chunks) - 1), acc=acc,
                                                 rq=[(tag, "qz", i, qi, x) for x in range(4)] + [(tag, "ksb", i, 0), (tag, "ksb", i, 1)],
                                                 rv=[(tag, "vaug", i, 0), (tag, "vaug", i, 1)], fin=fin))
                    self.attention_stream(jobs, pbufs, tag)
                    for ql in range(2):
                        qc = 2 * kv + ql
                        self.dma("sp", self.MIXT[qc * P:(qc + 1) * P, lat0:lat0 + L], ost[i][:, ql, 0:L], okeys,
                                 [("MIXT", "gqa", b, qc)])
                        if with_ctx:
                            self.dma("sp", self.MIXT[qc * P:(qc + 1) * P, cx0:cx0 + C], ost[i][:, ql, L:NTK], okeys,
                                     [("MIXT", "gqac", b, qc)])
            self.bg_drain()
        pg.barrier()


_HC = {}


def host_consts():
    if _HC:
        return _HC
    f32 = np.float32
    p = np.arange(P)
    statm = np.zeros((P, 3, P), f32)
    statm[:, 0, :] = (p[:, None] // 64 == p[None, :] // 64)
    statm[:, 1, :] = ((p[:, None] // 32) % 2 == (p[None, :] // 32) % 2)
    statm[:, 2, :] = 0.5
    _HC["statm"] = statm.astype(NPBF)
    c = np.arange(64)
    ang = 2 * np.pi * np.outer(c, c) / 64.0
    ctab = np.zeros((P, 2, 256), np.float64)
    for idx, T in enumerate((L, C)):
        nrm = 1.0 / np.sqrt(T * 64.0)
        for g in range(2):
            ctab[g * 64:(g + 1) * 64, idx, g * 64:(g + 1) * 64] = np.cos(ang) * nrm
            ctab[g * 64:(g + 1) * 64, idx, 128 + g * 64:128 + (g + 1) * 64] = -np.sin(ang) * nrm
    _HC["ctab"] = ctab.astype(f32).astype(NPBF)
    t = np.arange(L, dtype=np.int64)
    m = (t[:, None] * t[None, :]) % L
    a = m.astype(np.float64) * (2 * np.pi / L)
    _HC["dftC"] = np.cos(a).astype(f32).astype(NPBF)
    _HC["dftS"] = np.sin(a).astype(f32).astype(NPBF)
    t2 = np.arange(C, dtype=np.int64)
    a2 = ((t2[:, None] * t2[None, :]) % C).astype(np.float64) * (2 * np.pi / C)
    d256 = np.stack([np.cos(a2), np.sin(a2)], axis=0)
    d256 = d256.reshape(2, 2, P, C).transpose(2, 0, 1, 3)
    _HC["dft256"] = np.ascontiguousarray(d256).astype(f32).astype(NPBF)
    tt = np.arange(L)
    inv_freq = (np.float32(10000.0) ** (-np.arange(16, dtype=f32) / np.float32(16))).astype(f32)
    row = (tt // GRID).astype(f32)[:, None] * inv_freq
    col = (tt % GRID).astype(f32)[:, None] * inv_freq
    ang2 = np.concatenate([row, col], axis=-1).astype(f32)
    cosv = np.cos(ang2).astype(f32)
    sinv = np.sin(ang2).astype(f32)
    cosT = np.ascontiguousarray(cosv.T[p % 32, :])
    sinS = np.ascontiguousarray(sinv.T[p % 32, :])
    sinS[64:] *= -1.0
    _HC["cosT"] = cosT.astype(f32)
    _HC["sinS"] = sinS.astype(f32)
    _HC["ident"] = np.eye(P, dtype=f32)
    return _HC


def shared_inputs(inp):
    f32 = np.float32
    hc = host_consts()
    p = np.arange(P)
    sh = dict(hc)
    sh["ngT"] = np.ascontiguousarray(inp["norm_g"].reshape(DEPTH, 3, KC, P).transpose(3, 0, 1, 2))
    sh["w_mod"] = inp["w_mod"]
    sh["bmodT"] = np.ascontiguousarray(inp["b_mod"].reshape(DEPTH, 72, P).transpose(2, 0, 1))
    sh["ffn_w_gu"] = inp["ffn_w_gu"]
    sh["ffn_w_down"] = inp["ffn_w_down"]
    sh["w_in_ab"] = inp["w_in_ab"]
    sh["w_out_ab"] = inp["w_out_ab"]
    sh["w_out_cd"] = inp["w_out_cd"]
    j = np.arange(32)
    cols = []
    for qc in range(4):
        A, B = 2 * qc, 2 * qc + 1
        cols += [A * 64 + 2 * j, B * 64 + 2 * j, A * 64 + 2 * j + 1, B * 64 + 2 * j + 1]
    for kv in range(2):
        base = 512 + kv * 64
        cols += [base + 2 * j, base + 2 * j, base + 2 * j + 1, base + 2 * j + 1]
    cols.append(np.arange(640, 1792))
    cols = np.concatenate(cols)
    sh["w_in_cdp"] = np.ascontiguousarray(inp["w_in_cd"][:, :, cols])
    sh["qkgA"] = np.ascontiguousarray(inp["qk_g_a"][:, :, p % 64].transpose(2, 0, 1))
    dimp = 2 * (p % 32) + (p // 64)
    sh["qkgD"] = np.ascontiguousarray(inp["qk_g_d"][:, :, dimp].transpose(2, 0, 1))
    NEG = f32(-1e30)
    kc = np.arange(64)[:, None]
    cq = np.arange(64)[None, :]
    cs = np.clip(cq - 8, 0, 48)
    colok = (kc >= cs) & (kc < cs + 16)
    dci = np.clip(kc - cq + 15, 0, 30)
    rpb = inp["rpb_a"]
    Tb = np.where(colok[None, None, None], rpb[:, :, :, dci], NEG).astype(f32)
    negb = np.full((2, NA_HEADS, 64, 64), NEG, f32)
    mid = [Tb[:, :, (11 - m) + 7] if 8 <= m <= 15 else negb for m in range(25)]
    aa = [Tb[:, :, (15 - m) + 7] if 8 <= m <= 22 else negb for m in range(31)]
    strips = np.concatenate(mid + aa, axis=-1)
    sh["strips"] = np.ascontiguousarray(strips)
    sh["vgrow"] = np.ascontiguousarray(np.broadcast_to(inp["v_g_c"][None], (P, 2, 512)))
    bs = np.tile(inp["b_s_c"], (1, 1, 4))
    sh["bsrow"] = np.ascontiguousarray(np.broadcast_to(bs[None], (P, 2, 4, 512)))
    sh["wsT"] = np.ascontiguousarray(inp["w_s_c"].transpose(3, 0, 1, 2))
    return sh


def make_core_inputs(core, inp, sh=None):
    if sh is None:
        sh = shared_inputs(inp)
    b0 = core * NB
    cvec = np.stack([inp["c"][b0], inp["c"][b0 + 1], inp["c_ctx"]], axis=-1)
    m = dict(sh)
    m["x"] = np.ascontiguousarray(inp["x"][b0:b0 + NB])
    m["ctx"] = np.ascontiguousarray(inp["ctx"][b0:b0 + NB])
    m["cT"] = np.ascontiguousarray(cvec.reshape(KC, P, 3).transpose(1, 0, 2))
    return m


_CACHE = {}


def get_program(debug=False, stages=None, nlayers=DEPTH):
    key = (debug, stages, nlayers)
    if key not in _CACHE:
        b = Builder(debug=debug, stages=stages, nlayers=nlayers)
        nc = b.build()
        _CACHE[key] = (nc, b)
    return _CACHE[key]


def kernel(**inputs):
    inp = {k: np.asarray(v) for k, v in inputs.items()}
    nc, b = get_program()
    in_maps = []
    sh = shared_inputs(inp)
    for core in range(8):
        m = make_core_inputs(core, inp, sh)
        in_maps.append({k: m[k] for k in b.inputs})
    res = run_bass_kernel_spmd(nc, in_maps, core_ids=list(range(8)))
    out = np.concatenate([np.asarray(r["out"]) for r in res.results], axis=0)
    return out.astype(np.float32)
```

```python
import contextlib
import numpy as np
import ml_dtypes
import concourse.bass as bass
import concourse.mybir as mybir
from concourse.bass_utils import run_bass_kernel_spmd

F32 = mybir.dt.float32
BF16 = mybir.dt.bfloat16
AF = mybir.ActivationFunctionType
ALU = mybir.AluOpType
NPBF = ml_dtypes.bfloat16

P = 128
D = 1024
KC = 8
NB = 2
L = 4096
C = 256
TN = 512
NLAT = NB * L
NTOK = NLAT + NB * C
NT = NTOK // TN
NTL = NLAT // TN
DFF = 2816
FC = DFF // P
DEPTH = 4
EPS = 1e-6
HD = 64
GRID = 64
NA_HEADS = 12
NA_W = 768
IN_EVEN = 2560
IN_ODD = 1792

SEM_EPOCH = 24000
DMA_POOL = 8


class Prog:
    ENGS = ("pe", "act", "dve", "pool", "sp")

    def __init__(self, nc):
        self.nc = nc
        self.ops = []
        self.state = {}
        self.last_op = {}
        self.dma_since_barrier = []
        self._uid = 0

    def uid(self, s):
        self._uid += 1
        return "%s_%d" % (s, self._uid)

    def add(self, eng, fn, reads=(), writes=(), dma=False):
        idx = len(self.ops)
        deps = set()
        for k in reads:
            st = self.state.get(k)
            if st is not None and st[0] is not None:
                deps.add(st[0])
        for k in writes:
            st = self.state.get(k)
            if st is not None:
                if st[0] is not None:
                    deps.add(st[0])
                deps.update(st[1].values())
                deps.update(st[2])
        for k in reads:
            st = self.state.get(k)
            if st is None:
                st = [None, {}, []]
                self.state[k] = st
            if dma:
                st[2].append(idx)
            else:
                st[1][eng] = idx
        for k in writes:
            self.state[k] = [idx, {}, []]
        deps.discard(idx)
        self.ops.append(dict(eng=eng, fn=fn, deps=deps, dma=dma))
        if dma:
            self.dma_since_barrier.append(idx)
        else:
            self.last_op[eng] = idx
        return idx

    def barrier(self):
        b = set(self.last_op.values()) | set(self.dma_since_barrier)
        self.dma_since_barrier = []
        for e in self.ENGS:
            idx = len(self.ops)
            self.ops.append(dict(eng=e, fn=None, deps=set(b), dma=False))
        self.last_op = {}

    def emit(self, stack):
        nc = self.nc
        ops = self.ops
        needed = set()
        for op in ops:
            needed |= op["deps"]
        eng_sems = {e: [] for e in self.ENGS}
        eng_cnt = {e: 0 for e in self.ENGS}
        dma_pool = {e: [] for e in self.ENGS}
        dma_rr = {e: 0 for e in self.ENGS}

        def new_sem(tag):
            return stack.enter_context(nc.semaphore(self.uid(tag)))

        for idx, op in enumerate(ops):
            e = op["eng"]
            if op["dma"]:
                pool = dma_pool[e]
                if len(pool) < DMA_POOL:
                    pool.append([new_sem("dq_" + e), 0])
                slot = dma_rr[e] % DMA_POOL
                dma_rr[e] += 1
                if pool[slot][1] * 16 + 16 > SEM_EPOCH:
                    pool[slot] = [new_sem("dq_" + e), 0]
                ent = pool[slot]
                op["prev"] = (ent[0], ent[1] * 16) if ent[1] > 0 else None
                ent[1] += 1
                op["sig"] = (ent[0], ent[1] * 16, None)
            elif idx in needed and op["fn"] is not None:
                if not eng_sems[e] or eng_cnt[e] >= SEM_EPOCH:
                    eng_sems[e].append(new_sem("es_" + e))
                    eng_cnt[e] = 0
                eng_cnt[e] += 1
                op["sig"] = (eng_sems[e][-1], eng_cnt[e], (e, len(eng_sems[e]) - 1))
            else:
                op["sig"] = None

        engobj = dict(pe=nc.tensor, act=nc.scalar, dve=nc.vector, pool=nc.gpsimd, sp=nc.sync)
        block = stack.enter_context(nc.Block())

        def run_engine(e, eobj):
            known = {}
            knowne = {}

            def wait(sig):
                sem, val, ek = sig
                if ek is not None:
                    pe_, ep = ek
                    cur = knowne.get(pe_, (-1, 0))
                    if (ep, val) <= cur:
                        return
                    knowne[pe_] = (ep, val)
                else:
                    if known.get(id(sem), 0) >= val:
                        return
                    known[id(sem)] = val
                eobj.wait_ge(sem, val)

            for idx, op in enumerate(ops):
                if op["eng"] != e:
                    continue
                for d in sorted(op["deps"]):
                    dop = ops[d]
                    if dop["sig"] is None:
                        continue
                    if (not dop["dma"]) and dop["eng"] == "pe" and e == "pe" and not op["dma"]:
                        continue
                    wait(dop["sig"])
                if op["dma"] and op["prev"] is not None:
                    sem, val = op["prev"]
                    if known.get(id(sem), 0) < val:
                        known[id(sem)] = val
                        eobj.wait_ge(sem, val)
                if op["fn"] is None:
                    continue
                ins = op["fn"](eobj)
                if op["sig"] is not None:
                    sem, val, ek = op["sig"]
                    ins.then_inc(sem, 16 if op["dma"] else 1)

        block.tensor(lambda t: run_engine("pe", t))
        block.scalar(lambda t: run_engine("act", t))
        block.vector(lambda t: run_engine("dve", t))
        block.gpsimd(lambda t: run_engine("pool", t))
        block.sync(lambda t: run_engine("sp", t))


class Builder:
    def __init__(self, debug=False, stages=None, nlayers=DEPTH):
        self.debug = debug
        self.stages = stages
        self.nlayers = nlayers
        self.nc = bass.Bass("TRN2", target_bir_lowering=False)
        self.pg = Prog(self.nc)
        self.inputs = {}
        self.scratch = {}

    def dram_in(self, name, shape, dt):
        t = self.nc.dram_tensor(name, list(shape), dt, kind="ExternalInput")
        self.inputs[name] = t
        return t.ap()

    def dram_scratch(self, name, shape, dt):
        if self.debug:
            t = self.nc.dram_tensor(name, list(shape), dt, kind="ExternalOutput")
        else:
            t = self.nc.dram_tensor(name, list(shape), dt)
        self.scratch[name] = t
        return t.ap()

    def sb(self, stack, name, shape, dt):
        return stack.enter_context(self.nc.sbuf_tensor(self.pg.uid(name), list(shape), dt))

    def dma(self, q, out, in_, reads, writes, **kw):
        def fn(e, out=out, in_=in_, kw=kw):
            return e.dma_start(out=out, in_=in_, **kw)
        return self.pg.add(q, fn, reads, writes, dma=True)

    def mm_group(self, out, pairs, reads, writes, **kw):
        def fn(e, out=out, pairs=pairs, kw=kw):
            n = len(pairs)
            ins = None
            for i, (l, r) in enumerate(pairs):
                ins = e.matmul(out, l, r, start=(i == 0), stop=(i == n - 1), **kw)
            return ins
        return self.pg.add("pe", fn, reads, writes)

    def act(self, out, in_, func, reads, writes, eng="act", **kw):
        def fn(e, out=out, in_=in_, func=func, kw=kw):
            return e.activation(out=out, in_=in_, func=func, **kw)
        return self.pg.add(eng, fn, reads, writes)

    def tt(self, out, in0, in1, op, reads, writes, eng="dve"):
        def fn(e, out=out, in0=in0, in1=in1, op=op):
            return e.tensor_tensor(out=out, in0=in0, in1=in1, op=op)
        return self.pg.add(eng, fn, reads, writes)

    def stt(self, out, in0, scalar, in1, op0, op1, reads, writes):
        def fn(e, out=out, in0=in0, scalar=scalar, in1=in1, op0=op0, op1=op1):
            return e.scalar_tensor_tensor(out=out, in0=in0, scalar=scalar, in1=in1, op0=op0, op1=op1)
        return self.pg.add("dve", fn, reads, writes)

    def ts(self, out, in0, s1, s2, op0, op1, reads, writes, eng="dve"):
        def fn(e, out=out, in0=in0, s1=s1, s2=s2, op0=op0, op1=op1):
            if op1 is None:
                return e.tensor_scalar(out=out, in0=in0, scalar1=s1, scalar2=None, op0=op0)
            return e.tensor_scalar(out=out, in0=in0, scalar1=s1, scalar2=s2, op0=op0, op1=op1)
        return self.pg.add(eng, fn, reads, writes)

    def copy(self, out, in_, reads, writes, eng="dve"):
        def fn(e, out=out, in_=in_):
            return e.tensor_copy(out=out, in_=in_)
        return self.pg.add(eng, fn, reads, writes)

    def memset(self, ap, val, writes, eng="dve"):
        def fn(e, ap=ap, val=val):
            return e.memset(ap, val)
        return self.pg.add(eng, fn, (), writes)

    def recip(self, out, in_, reads, writes):
        def fn(e, out=out, in_=in_):
            return e.reciprocal(out=out, in_=in_)
        return self.pg.add("dve", fn, reads, writes)

    def transpose(self, out, in_, ident, reads, writes):
        def fn(e, out=out, in_=in_, ident=ident):
            return e.transpose(out, in_, ident)
        return self.pg.add("pe", fn, reads, writes)

    def build(self):
        nc = self.nc
        pg = self.pg
        I = self.dram_in
        self.x_in = I("x", [NB, L, D], F32)
        self.ctx_in = I("ctx", [NB, C, D], F32)
        self.cT_in = I("cT", [P, KC, 3], F32)
        self.ngT_in = I("ngT", [P, DEPTH, 3, KC], F32)
        self.wmod_in = I("w_mod", [DEPTH, D, 9 * D], F32)
        self.bmodT_in = I("bmodT", [P, DEPTH, 72], F32)
        self.wgu_in = I("ffn_w_gu", [DEPTH, 2, D, 2 * DFF], F32)
        self.wdn_in = I("ffn_w_down", [DEPTH, 2, DFF, D], F32)
        self.ident_in = I("ident", [P, P], F32)
        self.win_ab_in = I("w_in_ab", [2, D, IN_EVEN], F32)
        self.wout_ab_in = I("w_out_ab", [2, D, D], F32)
        self.win_cd_in = I("w_in_cdp", [2, D, 1920], F32)
        self.wout_cd_in = I("w_out_cd", [2, D, D], F32)
        self.qkgA_in = I("qkgA", [P, 2, 2], F32)
        self.qkgD_in = I("qkgD", [P, 2, 2], F32)
        self.statm_in = I("statm", [P, 3, P], BF16)
        self.ctab_in = I("ctab", [P, 2, 256], BF16)
        self.strips_in = I("strips", [2, NA_HEADS, 64, 3584], F32)
        self.dftC = I("dftC", [L, L], BF16)
        self.dftS = I("dftS", [L, L], BF16)
        self.dft256 = I("dft256", [P, 2, 2, C], BF16)
        self.cosT_in = I("cosT", [P, L], F32)
        self.sinS_in = I("sinS", [P, L], F32)
        self.vgrow_in = I("vgrow", [P, 2, 512], F32)
        self.bsrow_in = I("bsrow", [P, 2, 4, 512], F32)
        self.wsT_in = I("wsT", [P, 2, 4, P], F32)
        self.out_ap = nc.dram_tensor("out", [NB, L, D], F32, kind="ExternalOutput").ap()
        self.XT = self.dram_scratch("XT", [D, NTOK], F32)
        self.QT = self.dram_scratch("QT", [NA_W, NTOK], BF16)
        self.KT = self.dram_scratch("KT", [NA_W, NTOK], BF16)
        self.V = self.dram_scratch("V", [NTOK, NA_W], BF16)
        self.U = self.dram_scratch("U", [NTOK, 512], BF16)
        self.MIXT = self.dram_scratch("MIXT", [D, NTOK], BF16)

        with contextlib.ExitStack() as gs:
            self.gs = gs
            self.psbig = gs.enter_context(nc.psum_tensor("psbig", [P, 8 * 512], F32))
            self.ps = [self.psbig[:, i * 512:(i + 1) * 512] for i in range(8)]
            self.ident = self.sb(gs, "ident", [P, P], F32)
            self.ones_bf = self.sb(gs, "ones", [P, P], BF16)
            self.silu_c = self.sb(gs, "siluc", [P, KC, 3], F32)
            self.ngT = self.sb(gs, "ngT", [P, DEPTH, 3, KC], F32)
            self.bmodT = self.sb(gs, "bmodT", [P, DEPTH, 72], F32)
            self.modraw2 = [self.sb(gs, "modraw", [P, 72, 3], F32) for _ in range(2)]
            self.modA2 = [self.sb(gs, "modA", [P, 3, KC, 3], F32) for _ in range(2)]
            self.modG2 = [self.sb(gs, "modG", [P, 3, KC, 3], F32) for _ in range(2)]
            self.modS2 = [self.sb(gs, "modS", [P, 3, KC, 3], F32) for _ in range(2)]
            self.bg = None
            self.dma("sp", self.ident[:], self.ident_in[:, :], (), ["ident"])
            self.dma("sp", self.silu_c[:], self.cT_in[:, :, :], (), ["siluc"])
            self.dma("sp", self.ngT[:], self.ngT_in[:, :, :, :], (), ["ngT"])
            self.dma("sp", self.bmodT[:], self.bmodT_in[:, :, :], (), ["bmodT"])
            self.memset(self.ones_bf[:], 1.0, ["ones"])
            self.ident_bf = self.sb(gs, "identbf", [P, P], BF16)
            self.statm = self.sb(gs, "statm", [P, 3, P], BF16)
            self.ctab = self.sb(gs, "ctab", [P, 2, 256], BF16)
            self.qkgA = self.sb(gs, "qkgA", [P, 2, 2], F32)
            self.qkgD = self.sb(gs, "qkgD", [P, 2, 2], F32)
            self.copy(self.ident_bf[:], self.ident[:], ["ident"], ["identbf"])
            self.dma("sp", self.statm[:], self.statm_in[:, :, :], (), ["statm"])
            self.dma("sp", self.ctab[:], self.ctab_in[:, :, :], (), ["ctab"])
            self.dma("sp", self.qkgA[:], self.qkgA_in[:, :, :], (), ["qkgA"])
            self.dma("sp", self.qkgD[:], self.qkgD_in[:, :, :], (), ["qkgD"])
            self.act(self.silu_c[:], self.silu_c[:], AF.Silu, ["siluc"], ["siluc"])
            self.silu_cb = self.sb(gs, "silucb", [P, KC, 4], BF16)
            self.copy(self.silu_cb[:, :, 0:3], self.silu_c[:], ["siluc"], ["silucb"])

            self.phase_input()
            for l in range(self.nlayers):
                par = l % 2
                self.modA, self.modG, self.modS = self.modA2[par], self.modG2[par], self.modS2[par]
                self.kA, self.kG, self.kS = ("modA", par), ("modG", par), ("modS", par)
                self.phase_ffn(l, 0)
                if self.stages == "ffn0" and l == self.nlayers - 1:
                    break
                if l % 2 == 0:
                    self.phase_inproj_even(l)
                    if self.stages == "inproj" and l == self.nlayers - 1:
                        break
                    self.phase_na(l)
                    self.phase_fnet(l)
                else:
                    self.phase_inproj_odd(l)
                    if self.stages == "inproj" and l == self.nlayers - 1:
                        break
                    self.phase_gqa(l)
                if self.stages == "mix" and l == self.nlayers - 1:
                    break
                self.phase_outproj(l)
                if self.stages == "outproj" and l == self.nlayers - 1:
                    break
                self.phase_ffn(l, 1, ntiles=(NT if l < DEPTH - 1 else NTL))
            self.phase_output()
            pg.barrier()
            pg.emit(gs)
        return nc

    def phase_input(self):
        pg = self.pg
        with contextlib.ExitStack() as st:
            xin = [self.sb(st, "xin", [P, 4, D], F32) for _ in range(2)]
            xo = [self.sb(st, "xo", [P, KC, TN], F32) for _ in range(2)]
            self.bg = self.mod_task(0, st)
            for t in range(NT):
                i = t % 2
                self.bg_step()
                self.bg_step()
                if t < NTL:
                    b, t0 = divmod(t * TN, L)
                    src = self.x_in[b, t0:t0 + TN, :].rearrange("(s p) d -> p s d", p=P)
                    self.dma("sp", xin[i][:], src, (), [("xin", i, 0), ("xin", i, 1)])
                else:
                    for b in range(NB):
                        src = self.ctx_in[b, :, :].rearrange("(s p) d -> p s d", p=P)
                        self.dma("sp", xin[i][:, 2 * b:2 * b + 2, :], src, (), [("xin", i, b)])
                rk = [("xin", i, 0), ("xin", i, 1)]
                for k in range(KC):
                    bank = k % 4
                    for s in range(4):
                        self.transpose(self.ps[bank][:, s * P:(s + 1) * P], xin[i][:, s, k * P:(k + 1) * P],
                                       self.ident[:], rk + ["ident"], [("ps", bank)])
                    if k % 2 == 0:
                        self.copy(xo[i][:, k, :], self.ps[bank][:], [("ps", bank)], [("xo", i, k)], eng="dve")
                    else:
                        self.act(xo[i][:, k, :], self.ps[bank][:], AF.Copy, [("ps", bank)], [("xo", i, k)])
                dst = self.XT[:, t * TN:(t + 1) * TN].rearrange("(k p) n -> p k n", p=P)
                self.dma("sp", dst, xo[i][:], [("xo", i, k) for k in range(KC)],
                         [("XT", t, k) for k in range(KC)])
            self.bg_drain()
        pg.barrier()

    def phase_output(self):
        pg = self.pg
        with contextlib.ExitStack() as st:
            xi = [self.sb(st, "oxi", [P, KC, TN], F32) for _ in range(2)]
            xo = [self.sb(st, "oxo", [P, 4, D], F32) for _ in range(2)]
            for t in range(NTL):
                i = t % 2
                src = self.XT[:, t * TN:(t + 1) * TN].rearrange("(k p) n -> p k n", p=P)
                self.dma("sp", xi[i][:], src, [("XT", t, k) for k in range(KC)], [("oxi", i)])
                for s in range(4):
                    for kk in range(2):
                        bank = (s * 2 + kk) % 4
                        for k4 in range(4):
                            k = kk * 4 + k4
                            self.transpose(self.ps[bank][:, k4 * P:(k4 + 1) * P], xi[i][:, k, s * P:(s + 1) * P],
                                           self.ident[:], [("oxi", i), "ident"], [("ps", bank)])
                        if kk == 0:
                            self.copy(xo[i][:, s, kk * 512:(kk + 1) * 512], self.ps[bank][:], [("ps", bank)],
                                      [("oxo", i, s, kk)], eng="dve")
                        else:
                            self.act(xo[i][:, s, kk * 512:(kk + 1) * 512], self.ps[bank][:], AF.Copy,
                                     [("ps", bank)], [("oxo", i, s, kk)])
                b, t0 = divmod(t * TN, L)
                dst = self.out_ap[b, t0:t0 + TN, :].rearrange("(s p) d -> p s d", p=P)
                self.dma("sp", dst, xo[i][:], [("oxo", i, s, kk) for s in range(4) for kk in range(2)],
                         [("out", t)])
        pg.barrier()

    def bg_step(self):
        if self.bg is not None:
            try:
                next(self.bg)
            except StopIteration:
                self.bg = None

    def bg_drain(self):
        while self.bg is not None:
            self.bg_step()

    def mod_task(self, l, st):
        par = l % 2
        NBLK = 18
        wm = [self.sb(st, "wm", [P, KC, 512], BF16) for _ in range(2)]
        modraw = self.modraw2[par]
        ps6 = self.ps[6]
        ps6v = ps6[:, 0:12].rearrange("p (j n) -> p j n", n=3)

        def load(jb):
            src = self.wmod_in[l, :, jb * 512:(jb + 1) * 512].rearrange("(k p) j -> p k j", p=P)
            self.dma("pool", wm[jb % 2][:], src, (), [("wm", l, jb % 2)], max_dma_last_dim=4096)

        load(0)
        yield
        for jb in range(NBLK):
            if jb + 1 < NBLK:
                load(jb + 1)
            for j in range(4):
                pairs = [(wm[jb % 2][:, k, j * P:(j + 1) * P], self.silu_cb[:, k, 0:3]) for k in range(KC)]
                self.mm_group(ps6[:, j * 3:j * 3 + 3], pairs, [("wm", l, jb % 2), "silucb"], [("ps", 6)])
            for n in range(3):
                self.tt(modraw[:, jb * 4:jb * 4 + 4, n], ps6v[:, :, n], self.bmodT[:, l, jb * 4:jb * 4 + 4], ALU.add,
                        [("ps", 6), "bmodT"], [("modraw", par, jb, n)])
            yield
        allk = [("modraw", par, jb, n) for jb in range(NBLK) for n in range(3)]
        for s in range(3):
            for n in range(3):
                self.stt(self.modA2[par][:, s, :, n], modraw[:, s * 24 + 8:s * 24 + 16, n], 1.0,
                         self.ngT[:, l, s, :], ALU.add, ALU.mult, allk + ["ngT"], [("modA", par)])
            self.ts(self.modG2[par][:, s, :, :], modraw[:, s * 24 + 16:s * 24 + 24, :],
                    (1.0 if s == 1 else 0.5), None, ALU.mult, None, allk, [("modG", par)])
            self.copy(self.modS2[par][:, s, :, :], modraw[:, s * 24:s * 24 + 8, :], allk, [("modS", par)])
        yield

    def tile_n(self, t):
        return (t * TN) // L if t < NTL else 2

    def pre1(self, t, xa, hbuf, xk, hk):
        self.pre1_load(t, xa, xk)
        self.pre1_sq(xa, hbuf, xk, hk)

    def pre1_load(self, t, xa, xk):
        src = self.XT[:, t * TN:(t + 1) * TN].rearrange("(k p) n -> p k n", p=P)
        self.dma("sp", xa[:], src, [("XT", t, k) for k in range(KC)], xk)

    def pre1_sq(self, xa, hbuf, xk, hk):
        self.act(hbuf[:], xa[:], AF.Square, xk, hk)

    def pre2(self, t, s, xa, hbuf, xk, hk, rstd, rk, psbank, tt_eng="dve"):
        n = self.tile_n(t)
        pairs = [(self.ones_bf[:], hbuf[:, k, :]) for k in range(KC)]
        self.mm_group(self.ps[psbank][:], pairs, ["ones"] + hk, [("ps", psbank)])
        self.act(rstd[:], self.ps[psbank][:], AF.Ln, [("ps", psbank)], [rk], scale=1.0 / D, bias=EPS)
        self.act(rstd[:], rstd[:], AF.Exp, [rk], [rk], scale=-0.5)
        for k in range(KC):
            self.tt(xa[:, k, :], xa[:, k, :], rstd[:], ALU.mult, [xk[k], rk], [xk[k]],
                    eng=(tt_eng if k % 2 == 0 else "dve"))
            self.act(hbuf[:, k, :], xa[:, k, :], AF.Identity, [xk[k], self.kA, self.kS], [hk[k]],
                     scale=self.modA[:, s, k, n:n + 1], bias=self.modS[:, s, k, n:n + 1])

    def phase_ffn(self, l, which, ntiles=NT):
        pg = self.pg
        s = 0 if which == 0 else 2
        tag = "f%d_%d" % (l, which)
        with contextlib.ExitStack() as st:
            wgu = self.sb(st, "wgu", [P, KC, 2 * DFF], BF16)
            wdn = self.sb(st, "wdn", [P, FC, D], BF16)
            xa = self.sb(st, "xa", [P, KC, TN], F32)
            hb = [self.sb(st, "h", [P, KC, TN], BF16) for _ in range(2)]
            actb = self.sb(st, "actb", [P, FC, TN], BF16)
            xr = [self.sb(st, "xr", [P, TN], F32) for _ in range(2)]
            sg = [self.sb(st, "sg", [P, TN], F32) for _ in range(2)]
            rstd = self.sb(st, "rstd", [P, TN], F32)
            gu_src = self.wgu_in[l, which].rearrange("(k p) c -> p k c", p=P)
            for blk in range(FC // 2):
                for hh in range(2):
                    c0 = hh * DFF + blk * 256
                    self.dma("pool", wgu[:, :, c0:c0 + 256], gu_src[:, :, c0:c0 + 256], (), [(tag, "wgu", hh, blk)],
                             max_dma_last_dim=4096)
            for j in range(FC):
                src = self.wdn_in[l, which, j * P:(j + 1) * P, :]
                self.dma("pool", wdn[:, j, :], src, (), [(tag, "wdn", j)], max_dma_last_dim=4096)
            wdn_keys = [(tag, "wdn", j) for j in range(FC)]
            act_keys = [(tag, "act", j) for j in range(FC)]

            xk = [(tag, "xa", k) for k in range(KC)]
            hks = [[(tag, "h", ii, k) for k in range(KC)] for ii in range(2)]
            rk = (tag, "rstd")
            self.pre1(0, xa, hb[0], xk, hks[0])
            self.pre2(0, s, xa, hb[0], xk, hks[0], rstd, rk, 6)
            for t in range(ntiles):
                i = t % 2
                n = self.tile_n(t)
                hcur = hb[i]
                hk = hks[i]
                for j in range(FC):
                    pj = j % 2
                    bg, bu = 2 * pj, 2 * pj + 1
                    pairs = [(wgu[:, k, j * P:(j + 1) * P], hcur[:, k, :]) for k in range(KC)]
                    self.mm_group(self.ps[bg][:], pairs, [(tag, "wgu", 0, j // 2)] + hk, [("ps", bg)])
                    pairs = [(wgu[:, k, DFF + j * P:DFF + (j + 1) * P], hcur[:, k, :]) for k in range(KC)]
                    self.mm_group(self.ps[bu][:], pairs, [(tag, "wgu", 1, j // 2)] + hk, [("ps", bu)])
                    self.act(sg[pj][:], self.ps[bg][:], AF.Silu, [("ps", bg)], [(tag, "sg", pj)])
                    self.tt(actb[:, j, :], sg[pj][:], self.ps[bu][:], ALU.mult,
                            [(tag, "sg", pj), ("ps", bu)], [(tag, "act", j)])
                    if j == 15 and t + 1 < ntiles:
                        self.pre1(t + 1, xa, hb[1 - i], xk, hks[1 - i])
                if t + 1 < ntiles:
                    self.pre2(t + 1, s, xa, hb[1 - i], xk, hks[1 - i], rstd, rk, 6)
                for c in range(KC):
                    bd = 4 + (c % 2)
                    q = c % 2
                    self.dma("sp", xr[q][:], self.XT[c * P:(c + 1) * P, t * TN:(t + 1) * TN],
                             [("XT", t, c)], [(tag, "xr", q)])
                    pairs = [(wdn[:, j, c * P:(c + 1) * P], actb[:, j, :]) for j in range(FC)]
                    self.mm_group(self.ps[bd][:], pairs, wdn_keys + act_keys, [("ps", bd)])
                    self.stt(xr[q][:], self.ps[bd][:], self.modG[:, s, c, n:n + 1], xr[q][:], ALU.mult, ALU.add,
                             [("ps", bd), self.kG, (tag, "xr", q)], [(tag, "xr", q)])
                    self.dma("sp", self.XT[c * P:(c + 1) * P, t * TN:(t + 1) * TN], xr[q][:],
                             [(tag, "xr", q)], [("XT", t, c)])
        pg.barrier()


    def mm_multi(self, items, reads, writes):
        def fn(e, items=items):
            ins = None
            for (o, l, r, st_, sp_) in items:
                ins = e.matmul(o, l, r, start=st_, stop=sp_)
            return ins
        return self.pg.add("pe", fn, reads, writes)

    def qk_norm(self, psb, statm, statkey, gcol, gkey, sqb, sqk, rbuf, rkey, ssb, out_ap, out_key, n=TN,
                out_eng="dve"):
        ps = self.ps
        self.act(sqb[:, 0:n], ps[psb][:, 0:n], AF.Square, [("ps", psb)], [sqk])
        self.mm_group(ps[ssb][:, 0:n], [(statm, sqb[:, 0:n])], [statkey, sqk], [("ps", ssb)])
        self.act(rbuf[:, 0:n], ps[ssb][:, 0:n], AF.Ln, [("ps", ssb)], [rkey], scale=1.0 / HD, bias=EPS)
        self.act(rbuf[:, 0:n], rbuf[:, 0:n], AF.Exp, [rkey], [rkey], scale=-0.5)
        self.stt(out_ap, ps[psb][:, 0:n], gcol, rbuf[:, 0:n], ALU.mult, ALU.mult,
                 [("ps", psb), gkey, rkey], [out_key])

    def attention_stream(self, jobs, pbufs, tag):
        ps = self.ps
        GROUPS = [(0, 1), (2, 3), (6, 7)]
        NG = len(GROUPS)
        AHEAD = NG - 1
        groups = []
        i = 0
        while i < len(jobs):
            if i + 1 < len(jobs) and jobs[i + 1]["n"] == jobs[i]["n"]:
                groups.append([i, i + 1])
                i += 2
            else:
                groups.append([i])
                i += 1
        ng = len(groups)

        def emit_qk(gi):
            banks = GROUPS[gi % NG]
            for slot, ji in enumerate(groups[gi]):
                jb = jobs[ji]
                sbk = banks[slot]
                items = []
                nq = len(jb["qk"])
                for a_, (cols, l_, r_) in enumerate(jb["qk"]):
                    lo, hi = cols
                    items.append((ps[sbk][:, lo:hi], l_, r_, a_ == 0, a_ == nq - 1))
                self.mm_multi(items, jb["rq"], [("ps", sbk)])

        def emit_rest(gi):
            banks = GROUPS[gi % NG]
            g = groups[gi]
            n = jobs[g[0]]["n"]
            m = len(g)
            pb = pbufs[gi % 3]
            pk = (tag, "pb", gi % 3)
            b0 = banks[0]
            src = self.psbig[:, b0 * 512:(b0 + m) * 512].rearrange("p (m n) -> p m n", n=512)[:, :, 0:n]
            self.act(pb[:, 0:m, 0:n], src, AF.Exp, [("ps", banks[q_]) for q_ in range(m)], [pk])
            for slot, ji in enumerate(g):
                jb = jobs[ji]
                self.mm_multi([(ps[jb["acc"]][:, 0:n], jb["v"], pb[:, slot, 0:n], jb["first"], jb["last"])],
                              jb["rv"] + [pk], [("ps", jb["acc"])])
                if jb["last"] and jb["fin"] is not None:
                    jb["fin"]()

        for gi in range(min(AHEAD, ng)):
            emit_qk(gi)
        for gi in range(ng):
            if gi + AHEAD < ng:
                emit_qk(gi + AHEAD)
            emit_rest(gi)

    def att_finalize(self, acc, n, rden, rkey, out_ap, out_key):
        ps = self.ps
        self.recip(rden[:, 0:n], ps[acc][64:128, 0:n], [("ps", acc)], [rkey])
        self.tt(out_ap, ps[acc][0:64, 0:n], rden[:, 0:n], ALU.mult, [("ps", acc), rkey], [out_key])

    def run_units(self, units, mid_hook=None, main=(0, 1, 2), cell=None):
        n = len(units)
        if cell is None:
            cell = [0]
        bank_of = {}
        for k in range(n + 2):
            if k < n:
                if units[k][3]:
                    bank_of[k] = main[cell[0] % len(main)]
                    cell[0] += 1
                else:
                    bank_of[k] = None
                if units[k][0] is not None:
                    units[k][0](bank_of[k])
            if 0 <= k - 1 < n and units[k - 1][1] is not None:
                units[k - 1][1](bank_of[k - 1])
            if 0 <= k - 2 < n and units[k - 2][2] is not None:
                units[k - 2][2](bank_of[k - 2])
            if mid_hook is not None and k == n // 2:
                mid_hook()

    def qk_norm_a(self, psb, statm, statkey, sqb, sqk, ssb, n=TN):
        ps = self.ps
        self.act(sqb[:, 0:n], ps[psb][:, 0:n], AF.Square, [("ps", psb)], [sqk])
        self.mm_group(ps[ssb][:, 0:n], [(statm, sqb[:, 0:n])], [statkey, sqk], [("ps", ssb)])

    def qk_norm_b(self, psb, gcol, gkey, rbuf, rkey, ssb, out_ap, out_key, n=TN):
        ps = self.ps
        self.act(rbuf[:, 0:n], ps[ssb][:, 0:n], AF.Ln, [("ps", ssb)], [rkey], scale=1.0 / HD, bias=EPS)
        self.act(rbuf[:, 0:n], rbuf[:, 0:n], AF.Exp, [rkey], [rkey], scale=-0.5)
        self.stt(out_ap, ps[psb][:, 0:n], gcol, rbuf[:, 0:n], ALU.mult, ALU.mult,
                 [("ps", psb), gkey, rkey], [out_key])

    def phase_inproj_even(self, l):
        pg = self.pg
        i_ = l // 2
        s = 1
        tag = "ie%d" % l
        with_ctx = l < DEPTH - 1
        ps = self.ps
        MAIN = [0, 1, 2, 5]
        SSB = [3, 4]
        R = 3
        with contextlib.ExitStack() as st:
            win = self.sb(st, "win", [P, KC, IN_EVEN], BF16)
            xa = [self.sb(st, "xa", [P, KC, TN], F32) for _ in range(2)]
            hb = [self.sb(st, "h", [P, KC, TN], BF16) for _ in range(2)]
            rstd = self.sb(st, "rstd", [P, TN], F32)
            sqb = [self.sb(st, "sqb", [P, TN], BF16) for _ in range(R)]
            rb = [self.sb(st, "rb", [P, TN], F32) for _ in range(R)]
            qst = [self.sb(st, "qst", [P, 6, TN], BF16) for _ in range(2)]
            kst = [self.sb(st, "kst", [P, 6, TN], BF16) for _ in range(2)]
            vst = [self.sb(st, "vst", [P, 4, NA_W], BF16) for _ in range(2)]
            zst = [self.sb(st, "zst", [P, 2, TN], BF16) for _ in range(2)]
            ust = [self.sb(st, "ust", [P, 4, 512], BF16) for _ in range(2)]
            gq = self.sb(st, "gq", [P, 1], F32)
            w_src = self.win_ab_in[i_].rearrange("(k p) c -> p k c", p=P)
            order = [(3 * NA_W, 256, "z")] + [(c0, 256, "qk%d" % (c0 // 256)) for c0 in range(0, 2 * NA_W, 256)] + \
                    [(2 * NA_W + c0, 256, "v%d" % (c0 // 256)) for c0 in range(0, NA_W, 256)]
            wkey = {}
            for (c0, w, nm) in order:
                self.dma("pool", win[:, :, c0:c0 + w], w_src[:, :, c0:c0 + w], (), [(tag, "win", nm)],
                         max_dma_last_dim=4096)
                for cc in range(c0, c0 + w, P):
                    wkey[cc] = (tag, "win", nm)
            self.ts(gq[:], self.qkgA[:, i_, 0:1], 0.125, None, ALU.mult, None, ["qkgA"], [(tag, "gq")])
            xks = [[(tag, "xa", ii, k) for k in range(KC)] for ii in range(2)]
            hks = [[(tag, "h", ii, k) for k in range(KC)] for ii in range(2)]
            rk = (tag, "rstd")
            cnt = [0, 0, 0]
            bankcell = [0]
            self.pre1(0, xa[0], hb[0], xks[0], hks[0])
            self.pre2(0, s, xa[0], hb[0], xks[0], hks[0], rstd, rk, 6)
            for t in range(NT):
                i = t % 2
                isctx = t >= NTL
                if t + 1 < NT:
                    self.pre1_load(t + 1, xa[1 - i], xks[1 - i])
                h = hb[i]
                hk = hks[i]
                cols = slice(t * TN, (t + 1) * TN)
                units = []

                def feat_unit(c0, post, mid=None):
                    def pe(bank, c0=c0):
                        pairs = [(win[:, k, c0:c0 + P], h[:, k, :]) for k in range(KC)]
                        self.mm_group(ps[bank][:], pairs, [wkey[c0]] + hk, [("ps", bank)])
                    units.append((pe, mid, post, True))

                do_f = (not isctx) or with_ctx
                if do_f:
                    for fc in range(2):
                        def post(bank, fc=fc):
                            self.copy(zst[i][:, fc, :], ps[bank][:], [("ps", bank)], [(tag, "zst", i, fc)], eng="dve")
                        feat_unit(3 * NA_W + fc * P, post)
                for which in range(2):
                    if which == 0 and isctx and not with_ctx:
                        continue
                    stg = qst[i] if which == 0 else kst[i]
                    sname = "qst" if which == 0 else "kst"
                    for qc in range(6):
                        x = cnt[1] % R
                        ssb = SSB[cnt[1] % 2]
                        cnt[1] += 1

                        def mid(bank, x=x, ssb=ssb):
                            self.qk_norm_a(bank, self.statm[:, 0, :], "statm", sqb[x], (tag, "sqb", x), ssb)

                        def post(bank, which=which, qc=qc, stg=stg, sname=sname, x=x, ssb=ssb):
                            gcol = gq[:] if which == 0 else self.qkgA[:, i_, 1:2]
                            gkey = (tag, "gq") if which == 0 else "qkgA"
                            self.qk_norm_b(bank, gcol, gkey, rb[x], (tag, "rb", x), ssb, stg[:, qc, :],
                                           (tag, sname, i, qc))
                            if qc == 5:
                                dram = self.QT if which == 0 else self.KT
                                dname = "QT" if which == 0 else "KT"
                                dst = dram[0:NA_W, cols].rearrange("(c p) n -> p c n", p=P)
                                self.dma("sp", dst, stg[:], [(tag, sname, i, q_) for q_ in range(6)], [(dname, t)])
                        feat_unit(which * NA_W + qc * P, post, mid)
                for sub in range(4):
                    for half in range(2):
                        c0 = 2 * NA_W + half * 512
                        w = 512 if half == 0 else 256

                        def pe(bank, sub=sub, c0=c0, w=w):
                            pa = [(h[:, k, sub * P:(sub + 1) * P], win[:, k, c0:c0 + w]) for k in range(KC)]
                            self.mm_group(ps[bank][:, 0:w], pa, [wkey[c0], wkey[c0 + w - P]] + hk, [("ps", bank)])

                        def post(bank, sub=sub, half=half, w=w):
                            o = vst[i][:, sub, half * 512:half * 512 + w]
                            if half == 0:
                                self.copy(o, ps[bank][:, 0:w], [("ps", bank)], [(tag, "vst", i, sub, half)], eng="dve")
                            else:
                                self.act(o, ps[bank][:, 0:w], AF.Copy, [("ps", bank)], [(tag, "vst", i, sub, half)])
                            if sub == 3 and half == 1:
                                dst = self.V[t * TN:(t + 1) * TN, 0:NA_W].rearrange("(s p) f -> p s f", p=P)
                                self.dma("sp", dst, vst[i][:],
                                         [(tag, "vst", i, s_, h_) for s_ in range(4) for h_ in range(2)], [("V", t)])
                        units.append((pe, None, post, True))
                if do_f:
                    ctab = self.ctab[:, 1 if isctx else 0, :]
                    for sub in range(4):
                        def pe(bank, sub=sub):
                            items = []
                            for fc in range(2):
                                items.append((ps[bank][:, fc * 256:(fc + 1) * 256], zst[i][:, fc, sub * P:(sub + 1) * P],
                                              ctab, True, True))
                            self.mm_multi(items, [(tag, "zst", i, 0), (tag, "zst", i, 1), "ctab"], [("ps", bank)])

                        def post(bank, sub=sub):
                            if sub % 2 == 0:
                                self.copy(ust[i][:, sub, :], ps[bank][:], [("ps", bank)], [(tag, "ust", i, sub)], eng="dve")
                            else:
                                self.act(ust[i][:, sub, :], ps[bank][:], AF.Copy, [("ps", bank)], [(tag, "ust", i, sub)])
                            if sub == 3:
                                dst = self.U[t * TN:(t + 1) * TN, :].rearrange("(s p) f -> p s f", p=P)
                                self.dma("sp", dst, ust[i][:], [(tag, "ust", i, s_) for s_ in range(4)], [("U", t)])
                        units.append((pe, None, post, True))

                def mid(t=t, i=i):
                    if t + 1 < NT:
                        self.pre1_sq(xa[1 - i], hb[1 - i], xks[1 - i], hks[1 - i])
                        self.pre2(t + 1, s, xa[1 - i], hb[1 - i], xks[1 - i], hks[1 - i], rstd, rk, 6, tt_eng="pool")
                self.run_units(units, mid, main=MAIN, cell=bankcell)
        pg.barrier()

    def na_bias_items(self, g, c):
        kr = 2 * c
        MID, A = 0, 1
        if g == 0:
            first = (0, 256, A, 15 - kr) if c <= 3 else (0, 256, MID, 0)
            return [first, (256, 512, MID, 15 - kr)]
        if g == 7:
            second = (320, 512, A, 76 - kr) if c >= 28 else (320, 512, MID, 0)
            return [(0, 320, MID, 67 - kr), second]
        return [(0, 512, MID, 11 - kr + 8 * g)]

    def phase_na(self, l):
        pg = self.pg
        i_ = l // 2
        tag = "na%d" % l
        with_ctx = l < DEPTH - 1
        ps = self.ps
        MIDW, AW = 1664, 2048
        NTK = L + C
        with contextlib.ExitStack() as st:
            qz = [self.sb(st, "qz", [P, 2, NTK], BF16) for _ in range(2)]
            ksb = [self.sb(st, "ksb", [P, NTK], BF16) for _ in range(2)]
            vaug = [self.sb(st, "vaug", [P, 34, 2, P], BF16) for _ in range(2)]
            bt = [self.sb(st, "bt", [P, 2, MIDW + AW], BF16) for _ in range(2)]
            ost = [self.sb(st, "ost", [P, NTK], BF16) for _ in range(2)]
            pbufs = [self.sb(st, "pb", [P, 2, TN], BF16) for _ in range(3)]
            rden = [self.sb(st, "rden", [64, TN], F32) for _ in range(2)]
            for i in range(2):
                self.memset(qz[i][:], 0.0, [(tag, "qz", i, hh, x) for hh in range(2) for x in range(2)], eng="pool")
                self.memset(vaug[i][:], 1.0, [(tag, "vaug", i, hh, x) for hh in range(2) for x in range(2)], eng="pool")
                self.memset(bt[i][:], -1e30, [(tag, "bt", i, hh, x) for hh in range(2) for x in range(4)], eng="pool")
            it = 0
            fcount = [0]
            for b in range(NB):
                lat0 = b * L
                cx0 = NLAT + b * C
                for hp in range(6):
                    i = it % 2
                    it += 1
                    for hh in range(2):
                        r0 = hp * P + hh * 64
                        self.dma("sp", qz[i][hh * 64:hh * 64 + 64, hh, 0:L], self.QT[r0:r0 + 64, lat0:lat0 + L],
                                 [("QT", t) for t in range(b * 8, b * 8 + 8)], [(tag, "qz", i, hh, 0)])
                        if with_ctx:
                            self.dma("sp", qz[i][hh * 64:hh * 64 + 64, hh, L:NTK], self.QT[r0:r0 + 64, cx0:cx0 + C],
                                     [("QT", NTL)], [(tag, "qz", i, hh, 1)])
                    self.dma("sp", ksb[i][:, 0:L], self.KT[hp * P:(hp + 1) * P, lat0:lat0 + L],
                             [("KT", t) for t in range(b * 8, b * 8 + 8)], [(tag, "ksb", i, 0)])
                    self.dma("sp", ksb[i][:, L:NTK], self.KT[hp * P:(hp + 1) * P, cx0:cx0 + C],
                             [("KT", NTL)], [(tag, "ksb", i, 1)])
                    for hh in range(2):
                        f0 = hp * P + hh * 64
                        self.dma("sp", vaug[i][:, 0:32, hh, 0:64],
                                 self.V[lat0:lat0 + L, f0:f0 + 64].rearrange("(c p) d -> p c d", p=P),
                                 [("V", t) for t in range(b * 8, b * 8 + 8)], [(tag, "vaug", i, hh, 0)])
                        self.dma("sp", vaug[i][:, 32:34, hh, 0:64],
                                 self.V[cx0:cx0 + C, f0:f0 + 64].rearrange("(c p) d -> p c d", p=P),
                                 [("V", NTL)], [(tag, "vaug", i, hh, 1)])
                        hd = hp * 2 + hh
                        self.dma("pool", bt[i][0:64, hh, 0:1600], self.strips_in[i_, hd, :, 0:1600], (),
                                 [(tag, "bt", i, hh, 0)], max_dma_last_dim=4096)
                        self.dma("pool", bt[i][64:128, hh, 64:1664], self.strips_in[i_, hd, :, 0:1600], (),
                                 [(tag, "bt", i, hh, 1)], max_dma_last_dim=4096)
                        self.dma("pool", bt[i][0:64, hh, MIDW:MIDW + 1984], self.strips_in[i_, hd, :, 1600:3584], (),
                                 [(tag, "bt", i, hh, 2)], max_dma_last_dim=4096)
                        self.dma("pool", bt[i][64:128, hh, MIDW + 64:MIDW + 2048], self.strips_in[i_, hd, :, 1600:3584], (),
                                 [(tag, "bt", i, hh, 3)], max_dma_last_dim=4096)
                    jobs = []
                    for hh in range(2):
                        for g in range(9 if with_ctx else 8):
                            acc = 4 + fcount[0] % 2
                            rd = rden[fcount[0] % 2]
                            rdk = (tag, "rden", fcount[0] % 2)
                            fcount[0] += 1
                            if g < 8:
                                n = TN
                                q0 = g * TN
                                chunks = [c for c in range(4 * g - 2, 4 * g + 6) if 0 <= c < 32] + [32, 33]
                            else:
                                n = C
                                q0 = L
                                chunks = [32, 33]
                            qrhs = qz[i][:, hh, q0:q0 + n]
                            okey = (tag, "ost", i, hh, g)

                            def fin(acc=acc, n=n, rd=rd, rdk=rdk, i=i, hh=hh, q0=q0, okey=okey):
                                self.att_finalize(acc, n, rd, rdk, ost[i][hh * 64:hh * 64 + 64, q0:q0 + n], okey)

                            for ci, c in enumerate(chunks):
                                qk = [((0, n), ksb[i][:, c * P:(c + 1) * P], qrhs)]
                                if c < 32 and g < 8:
                                    for (lo, hi, strip, blk) in self.na_bias_items(g, c):
                                        off = (0 if strip == 0 else MIDW) + blk * 64
                                        qk.append(((lo, hi), self.ident_bf[:], bt[i][:, hh, off:off + (hi - lo)]))
                                jobs.append(dict(qk=qk, n=n, v=vaug[i][:, c, hh, :], first=(ci == 0),
                                                 last=(ci == len(chunks) - 1), acc=acc,
                                                 rq=[(tag, "qz", i, hh, 0), (tag, "qz", i, hh, 1), (tag, "ksb", i, 0), (tag, "ksb", i, 1), "identbf"]
                                                 + [(tag, "bt", i, hh, x) for x in range(4)],
                                                 rv=[(tag, "vaug", i, hh, 0), (tag, "vaug", i, hh, 1)], fin=fin))
                    self.attention_stream(jobs, pbufs, tag)
                    okeys = [(tag, "ost", i, hh, g) for hh in range(2) for g in range(9 if with_ctx else 8)]
                    self.dma("sp", self.MIXT[hp * P:(hp + 1) * P, lat0:lat0 + L], ost[i][:, 0:L], okeys,
                             [("MIXT", "na", b, hp)])
                    if with_ctx:
                        self.dma("sp", self.MIXT[hp * P:(hp + 1) * P, cx0:cx0 + C], ost[i][:, L:NTK], okeys,
                                 [("MIXT", "nac", b, hp)])
            self.bg_drain()
        pg.barrier()

    def phase_fnet(self, l):
        pg = self.pg
        tag = "fn%d" % l
        with_ctx = l < DEPTH - 1
        ps = self.ps
        with contextlib.ExitStack() as st:
            usb = [self.sb(st, "usb", [P, 32, 512], BF16) for _ in range(NB)]
            tb = [self.sb(st, "tb", [P, 32, TN], BF16) for _ in range(2)]
            yst = [self.sb(st, "yst", [P, 4, TN], BF16) for _ in range(2)]
            for b in range(NB):
                src = self.U[b * L:(b + 1) * L, :].rearrange("(c p) f -> p c f", p=P)
                self.dma("sp", usb[b][:], src, [("U", t) for t in range(b * 8, b * 8 + 8)], [(tag, "usb", b)])
            steps = [(j, part) for j in range(8) for part in range(2)]

            def load(si):
                j, part = steps[si]
                tab = self.dftC if part == 0 else self.dftS
                src = tab[:, j * TN:(j + 1) * TN].rearrange("(c p) n -> p c n", p=P)
                self.dma("sp", tb[si % 2][:], src, (), [(tag, "tb", si % 2)])

            load(0)
            for si, (j, part) in enumerate(steps):
                bset = (j % 2) * 4
                tbi = si % 2
                if si + 1 < len(steps):
                    load(si + 1)
                for c in range(32):
                    items = []
                    for b in range(NB):
                        for fc in range(2):
                            o0 = fc * 256 + part * P
                            items.append((ps[bset + b * 2 + fc][:], usb[b][:, c, o0:o0 + P], tb[tbi][:, c, :],
                                          part == 0 and c == 0, part == 1 and c == 31))
                    self.mm_multi(items, [(tag, "usb", 0), (tag, "usb", 1), (tag, "tb", tbi)],
                                  [("ps", bset + q) for q in range(4)])
                if part == 0:
                    continue
                yi = j % 2
                for q in range(4):
                    if q % 2 == 0:
                        self.copy(yst[yi][:, q, :], ps[bset + q][:], [("ps", bset + q)], [(tag, "yst", yi, q)], eng="dve")
                    else:
                        self.act(yst[yi][:, q, :], ps[bset + q][:], AF.Copy, [("ps", bset + q)], [(tag, "yst", yi, q)])
                for b in range(NB):
                    dst = self.MIXT[NA_W:D, b * L + j * TN:b * L + (j + 1) * TN].rearrange("(f p) n -> p f n", p=P)
                    self.dma("sp", dst, yst[yi][:, 2 * b:2 * b + 2, :], [(tag, "yst", yi, 2 * b), (tag, "yst", yi, 2 * b + 1)],
                             [("MIXT", "fn", b, j)])
            if with_ctx:
                usc = self.sb(st, "usc", [P, NB, 2, 512], BF16)
                tc_ = self.sb(st, "tc", [P, 2, 2, C], BF16)
                ysc = self.sb(st, "ysc", [P, NB, 2, C], BF16)
                for b in range(NB):
                    src = self.U[NLAT + b * C:NLAT + (b + 1) * C, :].rearrange("(c p) f -> p c f", p=P)
                    self.dma("sp", usc[:, b, :, :], src, [("U", NTL)], [(tag, "usc")])
                self.dma("sp", tc_[:], self.dft256[:, :, :, :], (), [(tag, "tc")])
                for b in range(NB):
                    for fc in range(2):
                        bank = b * 2 + fc
                        items = []
                        for part in range(2):
                            for c in range(2):
                                o0 = fc * 256 + part * P
                                items.append((ps[bank][:, 0:C], usc[:, b, c, o0:o0 + P], tc_[:, part, c, :],
                                              part == 0 and c == 0, part == 1 and c == 1))
                        self.mm_multi(items, [(tag, "usc"), (tag, "tc")], [("ps", bank)])
                        self.copy(ysc[:, b, fc, :], ps[bank][:, 0:C], [("ps", bank)], [(tag, "ysc", b, fc)], eng="dve")
                    dst = self.MIXT[NA_W:D, NLAT + b * C:NLAT + (b + 1) * C].rearrange("(f p) n -> p f n", p=P)
                    self.dma("sp", dst, ysc[:, b, :, :], [(tag, "ysc", b, 0), (tag, "ysc", b, 1)], [("MIXT", "fnc", b)])
        pg.barrier()

    def phase_outproj(self, l):
        pg = self.pg
        tag = "op%d" % l
        s = 1
        ps = self.ps
        ntiles = NT if l < DEPTH - 1 else NTL
        w_dram = self.wout_ab_in if l % 2 == 0 else self.wout_cd_in
        i_ = l // 2
        with contextlib.ExitStack() as st:
            wout = self.sb(st, "wout", [P, KC, D], BF16)
            mt = [self.sb(st, "mt", [P, KC, TN], BF16) for _ in range(2)]
            xr = [[self.sb(st, "xr", [P, TN], F32) for _ in range(KC)] for _ in range(2)]
            for k in range(KC):
                self.dma("pool", wout[:, k, :], w_dram[i_, k * P:(k + 1) * P, :], (), [(tag, "w", k)],
                         max_dma_last_dim=4096)
            wk = [(tag, "w", k) for k in range(KC)]
            if l + 1 < self.nlayers:
                self.bg = self.mod_task(l + 1, st)

            def loads(t):
                i = t % 2
                src = self.MIXT[:, t * TN:(t + 1) * TN].rearrange("(k p) n -> p k n", p=P)
                self.dma("sp", mt[i][:], src, [], [(tag, "mt", i)])
                for c in range(KC):
                    self.dma("sp", xr[i][c][:], self.XT[c * P:(c + 1) * P, t * TN:(t + 1) * TN],
                             [("XT", t, c)], [(tag, "xr", i, c)])

            loads(0)
            for t in range(ntiles):
                i = t % 2
                n = self.tile_n(t)
                if t + 1 < ntiles:
                    loads(t + 1)
                self.bg_step()
                self.bg_step()
                for c in range(KC):
                    bd = c % 4
                    pairs = [(wout[:, k, c * P:(c + 1) * P], mt[i][:, k, :]) for k in range(KC)]
                    self.mm_group(ps[bd][:], pairs, wk + [(tag, "mt", i)], [("ps", bd)])
                    self.stt(xr[i][c][:], ps[bd][:], self.modG[:, s, c, n:n + 1], xr[i][c][:], ALU.mult, ALU.add,
                             [("ps", bd), self.kG, (tag, "xr", i, c)], [(tag, "xr", i, c)])
                    self.dma("act", self.XT[c * P:(c + 1) * P, t * TN:(t + 1) * TN], xr[i][c][:],
                             [(tag, "xr", i, c)], [("XT", t, c)])
            self.bg_drain()
        pg.barrier()

    def phase_inproj_odd(self, l):
        pg = self.pg
        i_ = l // 2
        s = 1
        tag = "io%d" % l
        with_ctx = l < DEPTH - 1
        ps = self.ps
        WQ, WK, WV, WU, WS = 0, 512, 768, 896, 1408
        GELU = AF.Gelu_apprx_tanh
        MAIN = [0, 1, 2]
        SSB = [3, 4]
        MIXB = [5, 7, 6]
        R = 3
        with contextlib.ExitStack() as st:
            win = self.sb(st, "win", [P, KC, 1920], BF16)
            xa = [self.sb(st, "xa", [P, KC, TN], F32) for _ in range(2)]
            hb = [self.sb(st, "h", [P, KC, TN], BF16) for _ in range(2)]
            rstd = self.sb(st, "rstd", [P, TN], F32)
            sqb = [self.sb(st, "sqb", [P, TN], BF16) for _ in range(R)]
            rb = [self.sb(st, "rb", [P, TN], F32) for _ in range(R)]
            qn = [self.sb(st, "qn", [P, TN], F32) for _ in range(R)]
            t1 = [self.sb(st, "t1", [P, TN], F32) for _ in range(R)]
            t2 = [self.sb(st, "t2", [P, TN], F32) for _ in range(R)]
            qst = [self.sb(st, "qst", [P, 4, TN], BF16) for _ in range(2)]
            kst = [self.sb(st, "kst", [P, 2, TN], BF16) for _ in range(2)]
            vst = [self.sb(st, "vst", [P, 4, P], BF16) for _ in range(2)]
            ug = self.sb(st, "ug", [P, 4, TN], F32)
            gs = [self.sb(st, "gs", [P, 512], F32) for _ in range(4)]
            junk = self.sb(st, "junk", [P, P], BF16)
            ssq = [self.sb(st, "ssq", [P, 16], F32) for _ in range(2)]
            vn = [self.sb(st, "vn", [P, 512], BF16) for _ in range(4)]
            mtmp = [self.sb(st, "mtmp", [P, 4, P], F32) for _ in range(2)]
            mixst = [self.sb(st, "mixst", [P, 4, TN], BF16) for _ in range(2)]
            cosT = self.sb(st, "cosT", [P, L], F32)
            sinS = self.sb(st, "sinS", [P, L], F32)
            vgrow = self.sb(st, "vgrow", [P, 512], F32)
            bsrow = self.sb(st, "bsrow", [P, 4, P], F32)
            wsT = self.sb(st, "wsT", [P, 4, P], BF16)
            gq = self.sb(st, "gq", [P, 1], F32)
            w_src = self.win_cd_in[i_].rearrange("(k p) c -> p k c", p=P)
            wkey = {}
            for c0 in range(0, 1920, 256):
                w = min(256, 1920 - c0)
                self.dma("pool", win[:, :, c0:c0 + w], w_src[:, :, c0:c0 + w], (), [(tag, "win", c0)],
                         max_dma_last_dim=4096)
                for cc in range(c0, c0 + w, P):
                    wkey[cc] = (tag, "win", c0)
            self.dma("pool", wsT[:], self.wsT_in[:, i_, :, :], (), [(tag, "wsT")])
            self.dma("sp", cosT[:], self.cosT_in[:, :], (), [(tag, "cosT")])
            self.dma("sp", sinS[:], self.sinS_in[:, :], (), [(tag, "sinS")])
            self.dma("sp", vgrow[:], self.vgrow_in[:, i_, :], (), [(tag, "vgrow")])
            self.dma("sp", bsrow[:], self.bsrow_in[:, i_, :, 0:P], (), [(tag, "bsrow")])
            self.ts(gq[:], self.qkgD[:, i_, 0:1], 0.125, None, ALU.mult, None, ["qkgD"], [(tag, "gq")])
            xks = [[(tag, "xa", ii, k) for k in range(KC)] for ii in range(2)]
            hks = [[(tag, "h", ii, k) for k in range(KC)] for ii in range(2)]
            rk = (tag, "rstd")
            cnt = [0, 0, 0, 0]
            bankcell = [0]
            self.pre1(0, xa[0], hb[0], xks[0], hks[0])
            self.pre2(0, s, xa[0], hb[0], xks[0], hks[0], rstd, rk, 6)
            for t in range(NT):
                i = t % 2
                isctx = t >= NTL
                if t + 1 < NT:
                    self.pre1_load(t + 1, xa[1 - i], xks[1 - i])
                h = hb[i]
                hk = hks[i]
                cols = slice(t * TN, (t + 1) * TN)
                p0 = (t * TN) % L
                units = []

                def feat_unit(c0, post, mid=None):
                    def pe(bank, c0=c0):
                        pairs = [(win[:, k, c0:c0 + P], h[:, k, :]) for k in range(KC)]
                        self.mm_group(ps[bank][:], pairs, [wkey[c0]] + hk, [("ps", bank)])
                    units.append((pe, mid, post, True))

                do_sgu = (not isctx) or with_ctx
                if do_sgu:
                    for g in range(4):
                        def post(bank, g=g):
                            self.act(ug[:, g, :], ps[bank][:], GELU, [("ps", bank)], [(tag, "ug", g)])
                        feat_unit(WU + g * P, post)
                n_u = len(units)
                for which in range(2):
                    if which == 0 and isctx and not with_ctx:
                        continue
                    nch = 4 if which == 0 else 2
                    stg = qst[i] if which == 0 else kst[i]
                    sname = "qst" if which == 0 else "kst"
                    for qc in range(nch):
                        x = cnt[1] % R
                        ssb = SSB[cnt[1] % 2]
                        cnt[1] += 1

                        def mid(bank, which=which, x=x, ssb=ssb):
                            sm = self.statm[:, 1, :] if which == 0 else self.statm[:, 2, :]
                            self.qk_norm_a(bank, sm, "statm", sqb[x], (tag, "sqb", x), ssb)

                        def post(bank, which=which, qc=qc, stg=stg, sname=sname, nch=nch, x=x, ssb=ssb):
                            gcol = gq[:] if which == 0 else self.qkgD[:, i_, 1:2]
                            gkey = (tag, "gq") if which == 0 else "qkgD"
                            okey = (tag, sname, i, qc)
                            if isctx:
                                self.qk_norm_b(bank, gcol, gkey, rb[x], (tag, "rb", x), ssb, stg[:, qc, :], okey)
                            else:
                                qk_ = (tag, "qn", x)
                                self.qk_norm_b(bank, gcol, gkey, rb[x], (tag, "rb", x), ssb, qn[x][:], qk_)
                                self.tt(t1[x][:], qn[x][:], cosT[:, p0:p0 + TN], ALU.mult, [qk_, (tag, "cosT")],
                                        [(tag, "t1", x)], eng="dve")
                                self.tt(t2[x][0:64, :], qn[x][64:128, :], sinS[64:128, p0:p0 + TN], ALU.mult,
                                        [qk_, (tag, "sinS")], [(tag, "t2", x, 0)], eng="dve")
                                self.tt(t2[x][64:128, :], qn[x][0:64, :], sinS[0:64, p0:p0 + TN], ALU.mult,
                                        [qk_, (tag, "sinS")], [(tag, "t2", x, 1)], eng="dve")
                                self.tt(stg[:, qc, :], t1[x][:], t2[x][:], ALU.add,
                                        [(tag, "t1", x), (tag, "t2", x, 0), (tag, "t2", x, 1)], [okey])
                            if qc == nch - 1:
                                dram = self.QT if which == 0 else self.KT
                                dname = "QT" if which == 0 else "KT"
                                dst = dram[0:nch * P, cols].rearrange("(c p) n -> p c n", p=P)
                                self.dma("sp", dst, stg[:], [(tag, sname, i, q_) for q_ in range(nch)], [(dname, t)])
                        feat_unit((WQ if which == 0 else WK) + qc * P, post, mid)
                def pe_v(bank):
                    for sub in range(4):
                        pa = [(h[:, k, sub * P:(sub + 1) * P], win[:, k, WV:WV + P]) for k in range(KC)]
                        self.mm_group(ps[bank][:, sub * P:(sub + 1) * P], pa, [wkey[WV]] + hk, [("ps", bank)])

                def post_v(bank):
                    self.copy(vst[i][:].rearrange("p s f -> p (s f)"), ps[bank][:], [("ps", bank)], [(tag, "vst", i)], eng="dve")
                    dst = self.V[t * TN:(t + 1) * TN, 0:P].rearrange("(s p) f -> p s f", p=P)
                    self.dma("sp", dst, vst[i][:], [(tag, "vst", i)], [("V", t)])
                units.append((pe_v, None, post_v, True))
                if do_sgu:
                    s_units = []
                    mix_units = []
                    xs_ = []
                    for sub in range(4):
                        x = cnt[2] % 4
                        cnt[2] += 1
                        xs_.append(x)

                        def pe(bank, sub=sub):
                            pa = [(h[:, k, sub * P:(sub + 1) * P], win[:, k, WS:WS + 512]) for k in range(KC)]
                            self.mm_group(ps[bank][:], pa, [wkey[WS], wkey[WS + 256]] + hk, [("ps", bank)])

                        def post(bank, x=x, sub=sub):
                            self.act(gs[x][:], ps[bank][:], GELU, [("ps", bank)], [(tag, "gs", x)])
                            for g in range(4):
                                def fn(e, o=junk[:], a=gs[x][:, g * P:(g + 1) * P],
                                       acc=ssq[i][:, sub * 4 + g:sub * 4 + g + 1]):
                                    return e.activation(out=o, in_=a, func=AF.Square, accum_out=acc)
                                pg.add("act", fn, [(tag, "gs", x)], [(tag, "ssq", i, sub, g)])
                        s_units.append((pe, None, post, True))
                        mb = MIXB[cnt[3] % 3]
                        mi = cnt[3] % 2
                        cnt[3] += 1

                        def pe_m(bank, mb=mb, x=x):
                            items = [(ps[mb][:, g * P:(g + 1) * P], vn[x][:, g * P:(g + 1) * P], wsT[:, g, :], True, True)
                                     for g in range(4)]
                            self.mm_multi(items, [(tag, "vn", x, g) for g in range(4)] + [(tag, "wsT")], [("ps", mb)])

                        def post_m(bank, mb=mb, mi=mi, sub=sub):
                            self.tt(mtmp[mi][:].rearrange("p g n -> p (g n)"), ps[mb][:], bsrow[:].rearrange("p g n -> p (g n)"),
                                    ALU.add, [("ps", mb), (tag, "bsrow")], [(tag, "mtmp", mi)])
                            self.tt(mixst[i][:, :, sub * P:(sub + 1) * P], mtmp[mi][:], ug[:, :, sub * P:(sub + 1) * P], ALU.mult,
                                    [(tag, "mtmp", mi)] + [(tag, "ug", g) for g in range(4)], [(tag, "mixst", i, sub)])
                            if sub == 3:
                                dst = self.MIXT[512:D, cols].rearrange("(g p) n -> p g n", p=P)
                                self.dma("sp", dst, mixst[i][:], [(tag, "mixst", i, s_) for s_ in range(4)],
                                         [("MIXT", "sgu", t)])
                        mix_units.append((pe_m, None, post_m, False))

                    def post_fin(bank, xs_=xs_):
                        sk = [(tag, "ssq", i, sub, g) for sub in range(4) for g in range(4)]
                        self.act(ssq[i][:], ssq[i][:], AF.Ln, sk, sk, scale=1.0 / P, bias=EPS)
                        self.act(ssq[i][:], ssq[i][:], AF.Exp, sk, sk, scale=-0.5)
                        for sub in range(4):
                            x = xs_[sub]
                            for g in range(4):
                                self.stt(vn[x][:, g * P:(g + 1) * P], gs[x][:, g * P:(g + 1) * P],
                                         ssq[i][:, sub * 4 + g:sub * 4 + g + 1],
                                         vgrow[:, g * P:(g + 1) * P], ALU.mult, ALU.mult,
                                         [(tag, "gs", x), (tag, "vgrow")] + sk, [(tag, "vn", x, g)])
                    sgu_front = s_units + [(None, None, post_fin, False)]
                    sgu_back = mix_units
                else:
                    sgu_front, sgu_back = [], []
                units = units[:n_u] + sgu_front + units[n_u:] + sgu_back

                def mid(t=t, i=i):
                    if t + 1 < NT:
                        self.pre1_sq(xa[1 - i], hb[1 - i], xks[1 - i], hks[1 - i])
                        self.pre2(t + 1, s, xa[1 - i], hb[1 - i], xks[1 - i], hks[1 - i], rstd, rk, 6, tt_eng="pool")
                self.run_units(units, mid, main=MAIN, cell=bankcell)
        pg.barrier()

    def phase_gqa(self, l):
        pg = self.pg
        tag = "gq%d" % l
        with_ctx = l < DEPTH - 1
        ps = self.ps
        NTK = L + C
        with contextlib.ExitStack() as st:
            qz = [self.sb(st, "qz", [P, 4, NTK], BF16) for _ in range(2)]
            ksb = [self.sb(st, "ksb", [P, NTK], BF16) for _ in range(2)]
            vaug = [self.sb(st, "vaug", [P, 34, P], BF16) for _ in range(2)]
            ost = [self.sb(st, "ost", [P, 2, NTK], BF16) for _ in range(2)]
            pbufs = [self.sb(st, "pb", [P, 2, TN], BF16) for _ in range(3)]
            rden = [self.sb(st, "rden", [64, TN], F32) for _ in range(2)]
            for i in range(2):
                self.memset(qz[i][:], 0.0, [(tag, "qz", i, q, x) for q in range(4) for x in range(4)], eng="pool")
                self.memset(vaug[i][:], 1.0, [(tag, "vaug", i, 0), (tag, "vaug", i, 1)], eng="pool")
            it = 0
            fcount = 0
            for b in range(NB):
                lat0 = b * L
                cx0 = NLAT + b * C
                for kv in range(2):
                    i = it % 2
                    it += 1
                    latk = [("QT", t) for t in range(b * 8, b * 8 + 8)]
                    self.dma("sp", ksb[i][:, 0:L], self.KT[kv * P:(kv + 1) * P, lat0:lat0 + L], [], [(tag, "ksb", i, 0)])
                    self.dma("sp", ksb[i][:, L:NTK], self.KT[kv * P:(kv + 1) * P, cx0:cx0 + C], [], [(tag, "ksb", i, 1)])
                    self.dma("sp", vaug[i][:, 0:32, 0:64],
                             self.V[lat0:lat0 + L, kv * 64:kv * 64 + 64].rearrange("(c p) d -> p c d", p=P),
                             [], [(tag, "vaug", i, 0)])
                    self.dma("sp", vaug[i][:, 32:34, 0:64],
                             self.V[cx0:cx0 + C, kv * 64:kv * 64 + 64].rearrange("(c p) d -> p c d", p=P),
                             [], [(tag, "vaug", i, 1)])
                    for qi in range(4):
                        qc = 2 * kv + qi // 2
                        hh = qi % 2
                        for half in range(2):
                            r0 = half * 64 + hh * 32
                            self.dma("sp", qz[i][r0:r0 + 32, qi, 0:L], self.QT[qc * P + r0:qc * P + r0 + 32, lat0:lat0 + L],
                                     [], [(tag, "qz", i, qi, half)])
                            if with_ctx:
                                self.dma("sp", qz[i][r0:r0 + 32, qi, L:NTK],
                                         self.QT[qc * P + r0:qc * P + r0 + 32, cx0:cx0 + C], [], [(tag, "qz", i, qi, 2 + half)])
                    jobs = []
                    okeys = []
                    for qi in range(4):
                        hh = qi % 2
                        ql = qi // 2
                        for g in range(9 if with_ctx else 8):
                            acc = 4 + fcount % 2
                            rd = rden[fcount % 2]
                            rdk = (tag, "rden", fcount % 2)
                            fcount += 1
                            if g < 8:
                                n, q0 = TN, g * TN
                                chunks = list(range(34))
                            else:
                                n, q0 = C, L
                                chunks = [32, 33]
                            qrhs = qz[i][:, qi, q0:q0 + n]
                            okey = (tag, "ost", i, qi, g)
                            okeys.append(okey)

                            def fin(acc=acc, n=n, rd=rd, rdk=rdk, i=i, hh=hh, ql=ql, q0=q0, okey=okey):
                                self.att_finalize(acc, n, rd, rdk, ost[i][hh * 64:hh * 64 + 64, ql, q0:q0 + n], okey)

                            for ci, c in enumerate(chunks):
                                qk = [((0, n), ksb[i][:, c * P:(c + 1) * P], qrhs)]
                                jobs.append(dict(qk=qk, n=n, v=vaug[i][:, c, :], first=(ci == 0),
                                                 last=(ci == len(chunks) - 1), acc=acc,
                                                 rq=[(tag, "qz", i, qi, x) for x in range(4)] + [(tag, "ksb", i, 0), (tag, "ksb", i, 1)],
                                                 rv=[(tag, "vaug", i, 0), (tag, "vaug", i, 1)], fin=fin))
                    self.attention_stream(jobs, pbufs, tag)
                    for ql in range(2):
                        qc = 2 * kv + ql
                        self.dma("sp", self.MIXT[qc * P:(qc + 1) * P, lat0:lat0 + L], ost[i][:, ql, 0:L], okeys,
                                 [("MIXT", "gqa", b, qc)])
                        if with_ctx:
                            self.dma("sp", self.MIXT[qc * P:(qc + 1) * P, cx0:cx0 + C], ost[i][:, ql, L:NTK], okeys,
                                     [("MIXT", "gqac", b, qc)])
            self.bg_drain()
        pg.barrier()


_HC = {}


def host_consts():
    if _HC:
        return _HC
    f32 = np.float32
    p = np.arange(P)
    statm = np.zeros((P, 3, P), f32)
    statm[:, 0, :] = (p[:, None] // 64 == p[None, :] // 64)
    statm[:, 1, :] = ((p[:, None] // 32) % 2 == (p[None, :] // 32) % 2)
    statm[:, 2, :] = 0.5
    _HC["statm"] = statm.astype(NPBF)
    c = np.arange(64)
    ang = 2 * np.pi * np.outer(c, c) / 64.0
    ctab = np.zeros((P, 2, 256), np.float64)
    for idx, T in enumerate((L, C)):
        nrm = 1.0 / np.sqrt(T * 64.0)
        for g in range(2):
            ctab[g * 64:(g + 1) * 64, idx, g * 64:(g + 1) * 64] = np.cos(ang) * nrm
            ctab[g * 64:(g + 1) * 64, idx, 128 + g * 64:128 + (g + 1) * 64] = -np.sin(ang) * nrm
    _HC["ctab"] = ctab.astype(f32).astype(NPBF)
    t = np.arange(L, dtype=np.int64)
    m = (t[:, None] * t[None, :]) % L
    a = m.astype(np.float64) * (2 * np.pi / L)
    _HC["dftC"] = np.cos(a).astype(f32).astype(NPBF)
    _HC["dftS"] = np.sin(a).astype(f32).astype(NPBF)
    t2 = np.arange(C, dtype=np.int64)
    a2 = ((t2[:, None] * t2[None, :]) % C).astype(np.float64) * (2 * np.pi / C)
    d256 = np.stack([np.cos(a2), np.sin(a2)], axis=0)
    d256 = d256.reshape(2, 2, P, C).transpose(2, 0, 1, 3)
    _HC["dft256"] = np.ascontiguousarray(d256).astype(f32).astype(NPBF)
    tt = np.arange(L)
    inv_freq = (np.float32(10000.0) ** (-np.arange(16, dtype=f32) / np.float32(16))).astype(f32)
    row = (tt // GRID).astype(f32)[:, None] * inv_freq
    col = (tt % GRID).astype(f32)[:, None] * inv_freq
    ang2 = np.concatenate([row, col], axis=-1).astype(f32)
    cosv = np.cos(ang2).astype(f32)
    sinv = np.sin(ang2).astype(f32)
    cosT = np.ascontiguousarray(cosv.T[p % 32, :])
    sinS = np.ascontiguousarray(sinv.T[p % 32, :])
    sinS[64:] *= -1.0
    _HC["cosT"] = cosT.astype(f32)
    _HC["sinS"] = sinS.astype(f32)
    _HC["ident"] = np.eye(P, dtype=f32)
    return _HC


def shared_inputs(inp):
    f32 = np.float32
    hc = host_consts()
    p = np.arange(P)
    sh = dict(hc)
    sh["ngT"] = np.ascontiguousarray(inp["norm_g"].reshape(DEPTH, 3, KC, P).transpose(3, 0, 1, 2))
    sh["w_mod"] = inp["w_mod"]
    sh["bmodT"] = np.ascontiguousarray(inp["b_mod"].reshape(DEPTH, 72, P).transpose(2, 0, 1))
    sh["ffn_w_gu"] = inp["ffn_w_gu"]
    sh["ffn_w_down"] = inp["ffn_w_down"]
    sh["w_in_ab"] = inp["w_in_ab"]
    sh["w_out_ab"] = inp["w_out_ab"]
    sh["w_out_cd"] = inp["w_out_cd"]
    j = np.arange(32)
    cols = []
    for qc in range(4):
        A, B = 2 * qc, 2 * qc + 1
        cols += [A * 64 + 2 * j, B * 64 + 2 * j, A * 64 + 2 * j + 1, B * 64 + 2 * j + 1]
    for kv in range(2):
        base = 512 + kv * 64
        cols += [base + 2 * j, base + 2 * j, base + 2 * j + 1, base + 2 * j + 1]
    cols.append(np.arange(640, 1792))
    cols = np.concatenate(cols)
    sh["w_in_cdp"] = np.ascontiguousarray(inp["w_in_cd"][:, :, cols])
    sh["qkgA"] = np.ascontiguousarray(inp["qk_g_a"][:, :, p % 64].transpose(2, 0, 1))
    dimp = 2 * (p % 32) + (p // 64)
    sh["qkgD"] = np.ascontiguousarray(inp["qk_g_d"][:, :, dimp].transpose(2, 0, 1))
    NEG = f32(-1e30)
    kc = np.arange(64)[:, None]
    cq = np.arange(64)[None, :]
    cs = np.clip(cq - 8, 0, 48)
    colok = (kc >= cs) & (kc < cs + 16)
    dci = np.clip(kc - cq + 15, 0, 30)
    rpb = inp["rpb_a"]
    Tb = np.where(colok[None, None, None], rpb[:, :, :, dci], NEG).astype(f32)
    negb = np.full((2, NA_HEADS, 64, 64), NEG, f32)
    mid = [Tb[:, :, (11 - m) + 7] if 8 <= m <= 15 else negb for m in range(25)]
    aa = [Tb[:, :, (15 - m) + 7] if 8 <= m <= 22 else negb for m in range(31)]
    strips = np.concatenate(mid + aa, axis=-1)
    sh["strips"] = np.ascontiguousarray(strips)
    sh["vgrow"] = np.ascontiguousarray(np.broadcast_to(inp["v_g_c"][None], (P, 2, 512)))
    bs = np.tile(inp["b_s_c"], (1, 1, 4))
    sh["bsrow"] = np.ascontiguousarray(np.broadcast_to(bs[None], (P, 2, 4, 512)))
    sh["wsT"] = np.ascontiguousarray(inp["w_s_c"].transpose(3, 0, 1, 2))
    return sh


def make_core_inputs(core, inp, sh=None):
    if sh is None:
        sh = shared_inputs(inp)
    b0 = core * NB
    cvec = np.stack([inp["c"][b0], inp["c"][b0 + 1], inp["c_ctx"]], axis=-1)
    m = dict(sh)
    m["x"] = np.ascontiguousarray(inp["x"][b0:b0 + NB])
    m["ctx"] = np.ascontiguousarray(inp["ctx"][b0:b0 + NB])
    m["cT"] = np.ascontiguousarray(cvec.reshape(KC, P, 3).transpose(1, 0, 2))
    return m


_CACHE = {}


def get_program(debug=False, stages=None, nlayers=DEPTH):
    key = (debug, stages, nlayers)
    if key not in _CACHE:
        b = Builder(debug=debug, stages=stages, nlayers=nlayers)
        nc = b.build()
        _CACHE[key] = (nc, b)
    return _CACHE[key]


def kernel(**inputs):
    inp = {k: np.asarray(v) for k, v in inputs.items()}
    nc, b = get_program()
    in_maps = []
    sh = shared_inputs(inp)
    for core in range(8):
        m = make_core_inputs(core, inp, sh)
        in_maps.append({k: m[k] for k in b.inputs})
    res = run_bass_kernel_spmd(nc, in_maps, core_ids=list(range(8)))
    out = np.concatenate([np.asarray(r["out"]) for r in res.results], axis=0)
    return out.astype(np.float32)
```
